# Optimizing a Trainium2 kernel written in Bass

```python
import jax, jax.numpy as jnp
from jax import lax
import numpy as np

D_MODEL = 1024
BATCH = 16
SEQ = 4096
DEPTH = 1

CTX_LEN = 256
GRID_W = 64
D_MIX = D_MODEL
GLA_HEADS = 4
GLA_DK = 64
GLA_DV = 128
GLA_KEY = GLA_HEADS * GLA_DK
GLA_VAL = GLA_HEADS * GLA_DV
QKV_W = 2 * GLA_KEY + GLA_VAL
GATE_RANK = 16
GATE_NORMALIZER = 16.0
GLA_CHUNK = 64
CONV_K = 3
CMLP_HEADS = 8
CMLP_WIDTH = D_MIX - GLA_VAL
CMLP_CHUNK = 128
D_IN = QKV_W + 2 * GATE_RANK + GLA_VAL + 2 * CMLP_WIDTH
D_FF = 2816
ADA_MODS = 9
EPS = 1e-6

kernel_name = "hybrid_gla_gmlp_macaron_dit_block"


def rmsnorm(x, g):
    xf = x.astype(jnp.float32)
    n = xf * lax.rsqrt(jnp.mean(xf * xf, axis=-1, keepdims=True) + EPS)
    return (n * g.astype(jnp.float32)).astype(x.dtype)


def modulate(x, shift, scale):
    return x * (1 + scale) + shift


def adaln(cond, w, b):
    return (jax.nn.silu(cond) @ w + b).reshape(cond.shape[0], ADA_MODS, 1, D_MODEL)


def swiglu(y, w_in, w_out):
    gate, up = jnp.split(y @ w_in, 2, axis=-1)
    return (jax.nn.silu(gate) * up) @ w_out


def grid_dwconv(z, w, rows, cols):
    B, T, C = z.shape
    img = z.reshape(B, rows, cols, C)
    y = lax.conv_general_dilated(img, w[:, :, None, :].astype(z.dtype), (1, 1), 'SAME',
                                 dimension_numbers=('NHWC', 'HWIO', 'NHWC'),
                                 feature_group_count=C)
    return y.reshape(B, T, C)


def log_decay(z, w, b):
    B, T = z.shape[:2]
    a = jax.nn.log_sigmoid((z @ w + b).astype(jnp.float32)) / GATE_NORMALIZER
    return a.reshape(B, T, GLA_HEADS, GLA_DK)


def gla_scan(q, k, v, g, s0):
    B, T, H, _ = q.shape
    DV = v.shape[-1]
    nc = T // GLA_CHUNK

    def to_chunks(a):
        return a.reshape(B, nc, GLA_CHUNK, H, a.shape[-1]).transpose(1, 0, 3, 2, 4)

    lower = jnp.tril(jnp.ones((GLA_CHUNK, GLA_CHUNK), dtype=bool))[:, :, None]

    def step(s, xs):
        qc, kc, vc, gc = xs
        b = jnp.cumsum(gc, axis=2)
        rel = b[:, :, :, None, :] - b[:, :, None, :, :]
        decay = jnp.exp(jnp.where(lower, rel, -jnp.inf))
        scores = jnp.einsum('bhid,bhjd,bhijd->bhij', qc, kc, decay)
        o = (jnp.einsum('bhij,bhjv->bhiv', scores, vc)
             + jnp.einsum('bhid,bhdv->bhiv', qc * jnp.exp(b), s))
        b_last = b[:, :, -1:, :]
        s = (jnp.exp(b_last)[:, :, 0, :, None] * s
             + jnp.einsum('bhjd,bhjv->bhdv', kc * jnp.exp(b_last - b), vc))
        return s, o

    s, o = lax.scan(step, s0, (to_chunks(q), to_chunks(k), to_chunks(v), to_chunks(g)))
    o = o.transpose(1, 0, 3, 2, 4).reshape(B, T, H, DV)
    return o.astype(v.dtype), s


def gla_bidirectional(ctx_in, lat_in):
    qc, kc, vc, gfc, gbc = ctx_in
    qx, kx, vx, gfx, gbx = lat_in
    B = qc.shape[0]
    flip = lambda a: a[:, ::-1]
    s0 = jnp.zeros((B, GLA_HEADS, GLA_DK, GLA_DV), jnp.float32)
    o_cf, s_f = gla_scan(qc, kc, vc, gfc, s0)
    o_cb, s_b = gla_scan(flip(qc), flip(kc), flip(vc), flip(gbc), s0)
    o_xf, _ = gla_scan(qx, kx, vx, gfx, s_f)
    o_xb, _ = gla_scan(flip(qx), flip(kx), flip(vx), flip(gbx), s_b)
    return o_cf + flip(o_cb), o_xf + flip(o_xb)


def mixer_inputs(y, w_in, conv_w, w_gf, b_gf, w_gb, b_gb, rows, cols):
    B, T = y.shape[:2]
    z = y @ w_in
    i1 = QKV_W
    i2 = i1 + GATE_RANK
    i3 = i2 + GATE_RANK
    i4 = i3 + GLA_VAL
    i5 = i4 + CMLP_WIDTH
    qkv, glr_f, glr_b, og, u, vs = jnp.split(z, [i1, i2, i3, i4, i5], axis=-1)
    qkv = jax.nn.silu(grid_dwconv(qkv, conv_w, rows, cols))
    q, k, v = jnp.split(qkv, [GLA_KEY, 2 * GLA_KEY], axis=-1)
    q = q.reshape(B, T, GLA_HEADS, GLA_DK) * (GLA_DK ** -0.5)
    k = k.reshape(B, T, GLA_HEADS, GLA_DK)
    v = v.reshape(B, T, GLA_HEADS, GLA_DV)
    gla_in = (q, k, v, log_decay(glr_f, w_gf, b_gf), log_decay(glr_b, w_gb, b_gb))
    return gla_in, og, u, vs


def chunk_mlp(u, vs, norm_g, w_s, b_s):
    B, T, C = vs.shape
    u = jax.nn.gelu(u)
    vs = rmsnorm(jax.nn.gelu(vs), norm_g)
    vh = vs.reshape(B, T // CMLP_CHUNK, CMLP_CHUNK, CMLP_HEADS, C // CMLP_HEADS)
    mixed = jnp.einsum('hij,bnjhd->bnihd', w_s, vh) + b_s.T[None, None, :, :, None]
    return u * mixed.reshape(B, T, C)


def merge_out(o_gla, og, u, vs, gla_g, cm_g, w_s, b_s, w_out):
    B, T = og.shape[:2]
    a = rmsnorm(o_gla, gla_g).reshape(B, T, GLA_VAL) * jax.nn.silu(og)
    bm = chunk_mlp(u, vs, cm_g, w_s, b_s)
    return jnp.concatenate([a, bm], axis=-1) @ w_out


def setup_inputs(seed: int = 0) -> dict:
    key = jax.random.key(seed)
    ks = jax.random.split(key, 32)
    f32 = jnp.float32
    L, D = DEPTH, D_MODEL

    def nrm(k, shape, scale):
        return jax.random.normal(k, shape, f32) * scale

    return {
        "x": nrm(ks[0], (BATCH, SEQ, D), 1.0),
        "c": nrm(ks[1], (BATCH, D), 1.0),
        "ctx": nrm(ks[2], (BATCH, CTX_LEN, D), 1.0),
        "c_ctx": nrm(ks[3], (D,), 1.0),
        "w_ada": nrm(ks[4], (L, D, ADA_MODS * D), 0.5 * D ** -0.5),
        "b_ada": nrm(ks[5], (L, ADA_MODS * D), 0.02),
        "norm1_g": 1.0 + nrm(ks[6], (L, D), 0.02),
        "ff1_in": nrm(ks[7], (L, D, 2 * D_FF), D ** -0.5),
        "ff1_out": nrm(ks[8], (L, D_FF, D), D_FF ** -0.5),
        "norm2_g": 1.0 + nrm(ks[9], (L, D), 0.02),
        "w_in": nrm(ks[10], (L, D, D_IN), D ** -0.5),
        "conv_w": nrm(ks[11], (L, CONV_K, CONV_K, QKV_W), 1.0 / CONV_K),
        "w_gate_f": nrm(ks[12], (L, GATE_RANK, GLA_KEY), GATE_RANK ** -0.5),
        "b_gate_f": nrm(ks[13], (L, GLA_KEY), 0.1),
        "w_gate_b": nrm(ks[14], (L, GATE_RANK, GLA_KEY), GATE_RANK ** -0.5),
        "b_gate_b": nrm(ks[15], (L, GLA_KEY), 0.1),
        "gla_norm_g": 1.0 + nrm(ks[16], (L, GLA_DV), 0.02),
        "cmlp_norm_g": 1.0 + nrm(ks[17], (L, CMLP_WIDTH), 0.02),
        "w_s": nrm(ks[18], (L, CMLP_HEADS, CMLP_CHUNK, CMLP_CHUNK), CMLP_CHUNK ** -0.5),
        "b_s": 1.0 + nrm(ks[19], (L, CMLP_HEADS, CMLP_CHUNK), 0.02),
        "w_out": nrm(ks[20], (L, D_MIX, D), D_MIX ** -0.5),
        "norm3_g": 1.0 + nrm(ks[21], (L, D), 0.02),
        "ff2_in": nrm(ks[22], (L, D, 2 * D_FF), D ** -0.5),
        "ff2_out": nrm(ks[23], (L, D_FF, D), D_FF ** -0.5),
        "final_g": 1.0 + nrm(ks[24], (D,), 0.02),
    }


def reference(x, c, ctx, c_ctx, w_ada, b_ada, norm1_g, ff1_in, ff1_out, norm2_g, w_in,
              conv_w, w_gate_f, b_gate_f, w_gate_b, b_gate_b, gla_norm_g, cmlp_norm_g,
              w_s, b_s, w_out, norm3_g, ff2_in, ff2_out, final_g):
    rows = x.shape[1] // GRID_W
    h = ctx
    for l in range(DEPTH):
        last = l == DEPTH - 1
        m_x = adaln(c, w_ada[l], b_ada[l])
        m_c = adaln(c_ctx[None], w_ada[l], b_ada[l])

        x = x + 0.5 * m_x[:, 2] * swiglu(
            modulate(rmsnorm(x, norm1_g[l]), m_x[:, 0], m_x[:, 1]), ff1_in[l], ff1_out[l])
        h = h + 0.5 * m_c[:, 2] * swiglu(
            modulate(rmsnorm(h, norm1_g[l]), m_c[:, 0], m_c[:, 1]), ff1_in[l], ff1_out[l])

        y_x = modulate(rmsnorm(x, norm2_g[l]), m_x[:, 3], m_x[:, 4])
        y_c = modulate(rmsnorm(h, norm2_g[l]), m_c[:, 3], m_c[:, 4])
        gla_x, og_x, u_x, vs_x = mixer_inputs(y_x, w_in[l], conv_w[l], w_gate_f[l], b_gate_f[l],
                                              w_gate_b[l], b_gate_b[l], rows, GRID_W)
        gla_c, og_c, u_c, vs_c = mixer_inputs(y_c, w_in[l], conv_w[l], w_gate_f[l], b_gate_f[l],
                                              w_gate_b[l], b_gate_b[l], 1, CTX_LEN)
        o_c, o_x = gla_bidirectional(gla_c, gla_x)
        mix_x = merge_out(o_x, og_x, u_x, vs_x, gla_norm_g[l], cmlp_norm_g[l],
                          w_s[l], b_s[l], w_out[l])
        x = x + m_x[:, 5] * mix_x

        x = x + 0.5 * m_x[:, 8] * swiglu(
            modulate(rmsnorm(x, norm3_g[l]), m_x[:, 6], m_x[:, 7]), ff2_in[l], ff2_out[l])

        if not last:
            mix_c = merge_out(o_c, og_c, u_c, vs_c, gla_norm_g[l], cmlp_norm_g[l],
                              w_s[l], b_s[l], w_out[l])
            h = h + m_c[:, 5] * mix_c
            h = h + 0.5 * m_c[:, 8] * swiglu(
                modulate(rmsnorm(h, norm3_g[l]), m_c[:, 6], m_c[:, 7]), ff2_in[l], ff2_out[l])
    return rmsnorm(x, final_g)
```

```python
import numpy as np
import concourse.bass as bass
import concourse.mybir as mybir
from concourse.bass_utils import run_bass_kernel_spmd

F32 = mybir.dt.float32
BF16 = mybir.dt.bfloat16
AF = mybir.ActivationFunctionType
ALU = mybir.AluOpType

D = 1024
DFF = 2816
NFF = 22
CTX = 256
T = 256
EPS = 1e-6


class Buf:
    __slots__ = ("name", "ws", "readers", "pre", "open", "sem", "cnt")

    def __init__(self, name):
        self.name = name
        self.ws = []
        self.readers = []
        self.pre = []
        self.open = False
        self.sem = None
        self.cnt = 0


class PW:
    __slots__ = ("buf",)

    def __init__(self, buf):
        self.buf = buf


class Op:
    __slots__ = ("eng", "fn", "deps", "is_dma", "sem", "val", "signals", "idx")


def fence(newbuf, oldbufs):
    for b in oldbufs:
        newbuf.readers.extend(b.ws)
        newbuf.readers.extend(b.readers)
        if b.open:
            newbuf.readers.extend(b.pre)
    newbuf.open = False


class Sched:
    ENGS = ("pe", "act", "dve", "pool", "sp")

    def __init__(self, nc):
        self.nc = nc
        self.ops = {e: [] for e in self.ENGS}
        self.n = 0
        self.final = []
        self.dma_bufs = []
        self.cap = None

    def _add(self, eng, fn, reads, writes, is_dma, dbuf=None):
        op = Op()
        op.eng = eng
        op.fn = fn
        op.is_dma = is_dma
        op.signals = is_dma
        op.sem = None
        op.val = None
        op.idx = self.n
        self.n += 1
        deps = {}
        for b in reads:
            for w in b.ws:
                deps[id(w)] = (w, True)
        for wb in writes:
            if isinstance(wb, PW):
                b = wb.buf
                if not b.open:
                    b.pre = b.ws + b.readers
                    b.ws = []
                    b.readers = []
                    b.open = True
                for d in b.pre:
                    if id(d) not in deps:
                        deps[id(d)] = (d, False)
            else:
                for d in wb.ws + wb.readers:
                    if id(d) not in deps:
                        deps[id(d)] = (d, False)
        keep = []
        for d, raw in deps.values():
            if d is op:
                continue
            if (not is_dma) and (not d.is_dma) and d.eng == eng:
                if eng == "pe" or (not raw and eng != "pool"):
                    continue
            keep.append(d)
            d.signals = True
        op.deps = keep
        for b in reads:
            b.readers.append(op)
            b.open = False
        for wb in writes:
            if isinstance(wb, PW):
                wb.buf.ws.append(op)
            else:
                wb.ws = [op]
                wb.readers = []
                wb.open = False
        if is_dma:
            if dbuf.sem is None:
                self.dma_bufs.append(dbuf)
                dbuf.sem = True
            dbuf.cnt += 1
            op.sem = dbuf
            op.val = 16 * dbuf.cnt
        self.ops[eng].append(op)
        return op

    def op(self, eng, fn, reads=(), writes=()):
        if self.cap is not None:
            r, w = list(reads), list(writes)
            self.cap.append(lambda: self._add(eng, fn, r, w, False))
            return None
        return self._add(eng, fn, list(reads), list(writes), False)

    def dma(self, eng, fn, reads=(), writes=(), dbuf=None):
        return self._add(eng, fn, list(reads), list(writes), True, dbuf=dbuf)

    def emit(self):
        nc = self.nc
        esem = {e: nc.alloc_semaphore("s_" + e) for e in ("pe", "act", "dve", "pool")}
        for b in self.dma_bufs:
            b.sem = nc.alloc_semaphore("d_" + b.name)
        cnt = {e: 0 for e in esem}
        for e in self.ENGS:
            for o in self.ops[e]:
                if o.is_dma:
                    o.sem = o.sem.sem
                elif o.signals:
                    cnt[e] += 1
                    o.sem = esem[e]
                    o.val = cnt[e]
        final = self.final
        ops = self.ops

        def run(eng_name, h):
            waited = {}
            for o in ops[eng_name]:
                for d in o.deps:
                    key = d.sem.num
                    if waited.get(key, 0) >= d.val:
                        continue
                    h.wait_ge(d.sem, d.val)
                    waited[key] = d.val
                ins = o.fn(h)
                if o.signals:
                    ins.then_inc(o.sem, 16 if o.is_dma else 1)
            if eng_name == "sp":
                for d in final:
                    if waited.get(d.sem.num, 0) >= d.val:
                        continue
                    h.wait_ge(d.sem, d.val)
                    waited[d.sem.num] = d.val

        with nc.Block() as block:
            @block.tensor
            def _(h):
                run("pe", h)

            @block.scalar
            def _(h):
                run("act", h)

            @block.vector
            def _(h):
                run("dve", h)

            @block.gpsimd
            def _(h):
                run("pool", h)

            @block.sync
            def _(h):
                run("sp", h)


def build(NB, SEQ, dbg=False, stage=99):
    nc = bass.Bass("TRN2", target_bir_lowering=False)
    S = Sched(nc)
    NT = SEQ // T
    NCH = SEQ // 128
    ROWS = T // 64

    def din(name, shape):
        return nc.dram_tensor(name, list(shape), F32, kind="ExternalInput").ap()

    def dsc(name, shape, dt):
        return nc.dram_tensor(name, list(shape), dt, kind="Internal").ap()

    x = din("x", [NB, SEQ, D])
    c_in = din("c", [NB, D])
    ctx = din("ctx", [NB, CTX, D])
    c_ctx = din("c_ctx", [1, D])
    w_ada = din("w_ada", [D, 9 * D])
    b_ada = din("b_ada", [72, 128])
    gains = din("gains", [24, 128])
    ff_in = [din("ff1_in", [D, 2 * DFF]), din("ff2_in", [D, 2 * DFF])]
    ff_out = [din("ff1_out", [DFF, D]), din("ff2_out", [DFF, D])]
    w_in = din("w_in", [D, 2592])
    conv_w = din("conv_w", [9, D])
    w_gf = din("w_gate_f", [16, 256])
    b_gf = din("b_gate_f", [1, 256])
    w_gb = din("w_gate_b", [16, 256])
    b_gb = din("b_gate_b", [1, 256])
    gla_g = din("gla_norm_g", [128, 1])
    cm_g = din("cmlp_norm_g", [1, 512])
    w_s = din("w_s", [8, 128, 128])
    b_s = din("b_s", [8, 128])
    w_out = din("w_out", [D, D])
    final_g = din("final_g", [1, D])
    out = nc.dram_tensor("out", [NB, SEQ, D], F32, kind="ExternalOutput").ap()

    w2s = [[dsc("w2s0_%d" % r, [DFF, D], BF16) for r in range(3)], [dsc("w2s1_%d" % r, [DFF, D], BF16) for r in range(2)]]
    wins = dsc("wins", [10, 128, 8 * 256], BF16)
    x1s = dsc("x1s", [NB, SEQ, D], F32)
    x2s = dsc("x2s", [NB, SEQ, D], F32)
    zs = dsc("zs", [NB, 12, 128, SEQ], BF16)
    zcs = dsc("zcs", [NB, 8, 128, CTX], BF16)
    bms = dsc("bms", [NB, 4, 128, SEQ], BF16)
    glrs = dsc("glrs", [NB, 32, SEQ], F32)
    glrcs = dsc("glrcs", [NB, 32, CTX], F32)
    qks = dsc("qks", [NB, NT, 128, 2048], BF16)
    vts = dsc("vts", [NB, SEQ, 512], BF16)
    B_w2s = [[Buf("w2s0_%d" % r) for r in range(3)], [Buf("w2s1_%d" % r) for r in range(2)]]
    B_wins = Buf("wins")
    B_x1s = [Buf("x1s%d" % b) for b in range(NB)]
    B_x2s = [Buf("x2s%d" % b) for b in range(NB)]
    B_zs = [Buf("zs%d" % b) for b in range(NB)]
    B_zcs = [Buf("zcs%d" % b) for b in range(NB)]
    B_bms = [Buf("bms%d" % b) for b in range(NB)]
    B_glrs = [Buf("glrs%d" % b) for b in range(NB)]
    B_glrcs = [Buf("glrcs%d" % b) for b in range(NB)]
    B_qks = [Buf("qks%d" % b) for b in range(NB)]
    B_vts = [Buf("vts%d" % b) for b in range(NB)]

    def sb(name, shape, dt=F32):
        return nc.alloc_sbuf_tensor(name, list(shape), dt)

    def carver(base, start=0):
        state = [start]

        def take(shape, dt=F32):
            per = int(np.prod(shape[1:]))
            nb = per * (4 if dt == F32 else 2)
            o = state[0]
            state[0] += (nb + 31) // 32 * 32
            v = base[:, o // 2:(o + nb) // 2]
            if dt == F32:
                v = v.bitcast(F32)
            if len(shape) > 2:
                names = "abcd"[:len(shape) - 1]
                v = v.rearrange("p (%s) -> p %s" % (" ".join(names), " ".join(names)),
                                **{n: int(z) for n, z in zip(names, shape[1:])})
            return v[0:shape[0]] if shape[0] < 128 else v
        return take, state

    arena = sb("arena", [128, 8 * 2 * DFF], BF16)
    YB = 45568
    yreg = sb("yreg", [128, YB // 2], BF16)
    takeP, stP = carver(arena, 65536)
    takeA, stA = carver(yreg, 0)
    takeB, stB = carver(yreg, 0)
    takeC, stC = carver(yreg, 0)

    def MM(o, lhsT, rhs, st, sp, r, w):
        S.op("pe", lambda h: h.matmul(o, lhsT=lhsT, rhs=rhs, start=st, stop=sp), r, w)

    def ACT(o, i, func, r, w, scale=1.0, bias=None, accum=None):
        kw = {}
        if bias is not None:
            kw["bias"] = bias
        if accum is not None:
            kw["accum_out"] = accum
        S.op("act", lambda h: h.activation(out=o, in_=i, func=func, scale=scale, **kw), r, w)

    def TS(eng, o, i, s1, s2, op0, op1, r, w):
        if op1 is None:
            S.op(eng, lambda h: h.tensor_scalar(out=o, in0=i, scalar1=s1, scalar2=None, op0=op0), r, w)
        else:
            S.op(eng, lambda h: h.tensor_scalar(out=o, in0=i, scalar1=s1, scalar2=s2, op0=op0, op1=op1), r, w)

    def TT(eng, o, i0, i1, op, r, w):
        S.op(eng, lambda h: h.tensor_tensor(out=o, in0=i0, in1=i1, op=op), r, w)

    def STT(eng, o, i0, sc, i1, op0, op1, r, w):
        eng = "dve"
        S.op(eng, lambda h: h.scalar_tensor_tensor(out=o, in0=i0, scalar=sc, in1=i1, op0=op0, op1=op1), r, w)

    def CP(eng, o, i, r, w):
        if eng == "act":
            S.op("act", lambda h: h.copy(out=o, in_=i), r, w)
        else:
            S.op(eng, lambda h: h.tensor_copy(out=o, in_=i), r, w)

    def MS(eng, ap, val, w):
        S.op(eng, lambda h: h.memset(ap, val), [], w)

    def DMA(eng, o, i, r, w, dbuf):
        return S.dma(eng, lambda h: h.dma_start(out=o, in_=i), r, w, dbuf=dbuf)

    TP = [nc.alloc_psum_tensor("tp%d" % i, [128, 1024], BF16) for i in range(2)]
    B_TP = [Buf("tp%d" % i) for i in range(2)]
    F = [nc.alloc_psum_tensor("f%d" % i, [128, 512], F32) for i in range(6)]
    B_F = [Buf("f%d" % i) for i in range(6)]

    B_c = Buf("const")
    identf = sb("identf", [128, 128])
    identb = sb("identb", [128, 128], BF16)
    onesf = sb("onesf", [128, 128])
    onesb = sb("onesb", [128, 128], BF16)
    mf = sb("mf", [128, 128])
    mb = sb("mb", [128, 128])
    trif = sb("trif", [128, 128])
    trib = sb("trib", [128, 128])
    mf4 = sb("mf4", [128, 4, 128], BF16)
    mb4 = sb("mb4", [128, 4, 128], BF16)
    MS("pool", identf[:], 0.0, [B_c])
    S.op("pool", lambda h: h.affine_select(out=identf[:], in_=identf[:], pattern=[[-1, 128]], compare_op=ALU.not_equal,
                                           fill=1.0, base=0, channel_multiplier=1), [B_c], [B_c])
    CP("pool", identb[:], identf[:], [B_c], [B_c])
    MS("pool", onesf[:], 1.0, [B_c])
    MS("pool", onesb[:], 1.0, [B_c])
    MS("pool", mf[:], 1.0, [B_c])
    S.op("pool", lambda h: h.affine_select(out=mf[:], in_=mf[:], pattern=[[1, 128]], compare_op=ALU.is_ge,
                                           fill=0.0, base=0, channel_multiplier=-1), [B_c], [B_c])
    MS("pool", mb[:], 1.0, [B_c])
    S.op("pool", lambda h: h.affine_select(out=mb[:], in_=mb[:], pattern=[[-1, 128]], compare_op=ALU.is_ge,
                                           fill=0.0, base=0, channel_multiplier=1), [B_c], [B_c])
    TS("pool", trif[:], mf[:], -1.0 / 16.0, None, ALU.mult, None, [B_c], [B_c])
    TS("pool", trib[:], mb[:], -1.0 / 16.0, None, ALU.mult, None, [B_c], [B_c])
    for hh in range(4):
        CP("pool", mf4[:, hh, :], (mf if hh < 2 else mb)[:], [B_c], [B_c])

    W1 = arena[:, :].rearrange("p (k c) -> p k c", k=8)
    B_W1 = Buf("W1")

    B_par = Buf("params")
    crow = takeP([3, D])
    brow = takeP([72, 128])
    grow = takeP([24, 128])
    cwrow = takeP([9, D])
    bsrow = takeP([8, 128])
    fgrow = takeP([1, D])
    cmrow = takeP([1, 512])
    glag = sb("glag", [128, 1])
    Wg = sb("Wg", [33, 512])
    wssb = takeP([128, 8, 128])
    MS("pool", Wg[:], 0.0, [B_par])
    pl = []
    pl.append((crow[0:NB, :], c_in))
    pl.append((crow[2:3, :], c_ctx))
    pl.append((brow[:], b_ada))
    pl.append((grow[:], gains))
    pl.append((cwrow[:], conv_w))
    pl.append((bsrow[:], b_s))
    pl.append((fgrow[:], final_g))
    pl.append((cmrow[:], cm_g))
    pl.append((glag[:], gla_g))
    pl.append((Wg[0:16, 0:256], w_gf))
    pl.append((Wg[16:32, 256:512], w_gb))
    pl.append((Wg[32:33, 0:256], b_gf))
    pl.append((Wg[32:33, 256:512], b_gb))
    pl.append((wssb[:], w_s.rearrange("h i j -> i h j")))
    if NB < 2:
        MS("pool", crow[:], 0.0, [B_par])
    for (o, i) in pl:
        DMA("sp", o, i, [], [B_par], B_par)

    B_mod = Buf("mod")
    sc = crow
    scT = sb("scT", [128, 8, 4])
    mods = sb("mods", [128, 72, 3])
    bT = sb("bT", [128, 72])
    gT = sb("gT", [128, 24])
    AB = sb("AB", [128, 3, 3, 2, 8])
    cw = sb("cw", [128, 8, 9])
    bsT = sb("bsT", [128, 8])
    wsb = takeP([128, 8, 128], BF16)
    wsT = sb("wsT", [128, 8, 128], BF16)
    G = sb("G", [128, 3, D])
    FG = sb("FG", [128, D])
    B_G = Buf("G")
    CMG = sb("CMG", [128, 512])
    ACT(sc[:], crow[:], AF.Silu, [B_par], [B_mod])
    for k in range(8):
        MM(F[1][:, k * 4:k * 4 + 3], sc[0:3, k * 128:(k + 1) * 128], identf[0:3, 0:3], True, True, [B_mod, B_c], [B_F[1]])
    CP("dve", scT[:, :, 0:3], F[1][:, 0:32].rearrange("p (k r) -> p k r", r=4)[:, :, 0:3], [B_F[1]], [B_mod])
    MM(F[1][:, 0:72], brow[0:72, :], identf[0:72, 0:72], True, True, [B_par, B_c, B_mod], [B_F[1]])
    CP("dve", bT[:], F[1][:, 0:72], [B_F[1]], [B_mod])
    MM(F[1][:, 0:24], grow[0:24, :], identf[0:24, 0:24], True, True, [B_par, B_c], [B_F[1]])
    CP("dve", gT[:], F[1][:, 0:24], [B_F[1]], [B_mod])
    for cch in range(8):
        MM(F[1][:, cch * 16:cch * 16 + 9], cwrow[0:9, cch * 128:(cch + 1) * 128], identf[0:9, 0:9], True, True,
           [B_par, B_c], [B_F[1]])
    CP("dve", cw[:], F[1][:, 0:128].rearrange("p (c t) -> p c t", t=16)[:, :, 0:9], [B_F[1]], [B_mod])
    MM(F[1][:, 0:8], bsrow[0:8, :], identf[0:8, 0:8], True, True, [B_par, B_c], [B_F[1]])
    TS("dve", bsT[:], F[1][:, 0:8], 0.5, None, ALU.mult, None, [B_F[1]], [B_mod])
    MM(F[1][:, :], onesf[0:1, 0:128], cmrow[0:1, :], True, True, [B_par, B_c], [B_F[1]])
    CP("dve", CMG[:], F[1][:, :], [B_F[1]], [B_mod])
    TS("dve", wsb[:], wssb[:], 0.5, None, ALU.mult, None, [B_par], [B_mod])
    for hh in range(8):
        S.op("pe", (lambda hh: lambda h: h.transpose(out=TP[0][:, hh * 128:(hh + 1) * 128], in_=wsb[:, hh, :], identity=identb[:]))(hh),
             [B_mod, B_c], [B_TP[0]])
    CP("dve", wsT[:].rearrange("p h i -> p (h i)"), TP[0][:, :], [B_TP[0]], [B_mod])

    B_wa = [Buf("wa0"), Buf("wa1")]
    waf = arena[:, 0:2 * 16384].bitcast(F32)
    wa = [waf[:, i * 8192:(i + 1) * 8192].rearrange("p (k c) -> p k c", k=8) for i in range(2)]
    MTv = F[0][:, 0:288].rearrange("p (n r) -> p n r", r=4)
    mrow = takeP([3, 512])
    B_mrow = Buf("mrow")
    for m in range(9):
        for k in range(8):
            DMA("sp", wa[m % 2][:, k, :], w_ada[k * 128:(k + 1) * 128, m * D:(m + 1) * D], [], [B_wa[m % 2]], B_wa[m % 2])
        for hf in range(2):
            for k in range(8):
                MM(F[1][0:3, :], scT[:, k, 0:3], wa[m % 2][:, k, hf * 512:(hf + 1) * 512], k == 0, k == 7,
                   [B_wa[m % 2], B_mod], [B_F[1]])
            CP("act", mrow[0:3, :], F[1][0:3, :], [B_F[1]], [B_mrow])
            for cc in range(4):
                n = m * 8 + hf * 4 + cc
                MM(MTv[:, n, 0:3], mrow[0:3, cc * 128:(cc + 1) * 128], identf[0:3, 0:3], True, True, [B_mrow, B_c], [B_F[0]])
    for r in range(3):
        TT("dve", mods[:, :, r], MTv[:, :, r], bT[:], ALU.add, [B_F[0], B_mod], [B_mod])
    for n in range(3):
        for r in range(3):
            TS("dve", AB[:, n, r, 0, :], mods[:, (3 * n + 1) * 8:(3 * n + 2) * 8, r], 1.0, None, ALU.add, None, [B_mod], [B_mod])
            TT("dve", AB[:, n, r, 0, :], AB[:, n, r, 0, :], gT[:, n * 8:(n + 1) * 8], ALU.mult, [B_mod], [B_mod])
            CP("dve", AB[:, n, r, 1, :], mods[:, (3 * n) * 8:(3 * n + 1) * 8, r], [B_mod], [B_mod])

    dg = [sb("dg%d" % i, [128, 128]) for i in range(2)]
    B_dg = [Buf("dg0"), Buf("dg1")]

    def make_G(slot, gate_idx, r, coef):
        for k in range(8):
            TS("dve", dg[k % 2][:], identf[:], mods[:, gate_idx * 8 + k, r:r + 1], coef, ALU.mult, ALU.mult,
               [B_c, B_mod], [B_dg[k % 2]])
            MM(F[2 + k // 4][:, (k % 4) * 128:(k % 4 + 1) * 128], onesf[:], dg[k % 2][:], True, True,
               [B_c, B_dg[k % 2]], [B_F[2 + k // 4]])
        CP("act", G[:, slot, 0:512], F[2][:, :], [B_F[2]], [B_G])
        CP("act", G[:, slot, 512:1024], F[3][:, :], [B_F[3]], [B_G])

    def make_FG(slot):
        for hf in range(2):
            MM(F[2 + hf][:, :], onesf[0:1, 0:128], fgrow[0:1, hf * 512:(hf + 1) * 512], True, True, [B_par, B_c], [B_F[2 + hf]])
            CP("act", FG[:, hf * 512:(hf + 1) * 512], F[2 + hf][:, :], [B_F[2 + hf]], [B_mod])

    cl_i = [0]

    def cast_load(dst, src, view, wbufs, gslot=None):
        i = cl_i[0] % 2
        cl_i[0] += 1
        stv = view(xt[i][:, :, :].rearrange("p s d -> p (s d)"))
        DMA("sp", stv, src, [], [B_xt[i]], B_xt[i])
        if gslot is None:
            CP("act" if i == 0 else "dve", dst, stv, [B_xt[i]], wbufs)
        elif len(dst.shape) == 3:
            for jj in range(dst.shape[1]):
                TT("dve", dst[:, jj, :], stv[:, jj, :], G[:, gslot, :], ALU.mult, [B_xt[i], B_G], wbufs)
        else:
            TT("dve", dst, stv, G[:, gslot, :], ALU.mult, [B_xt[i], B_G], wbufs)

    def conv_w2(f, r, gslot):
        for g in range(11):
            i = cvt_i[0] % 2
            cvt_i[0] += 1
            cast_load(wst[i][:, :].rearrange("p (j c) -> p j c", j=2),
                      ff_out[f][g * 256:(g + 1) * 256, :].rearrange("(j p) c -> p j c", p=128),
                      lambda fl: fl.rearrange("p (j c) -> p j c", j=2), [PW(B_wst[i])], gslot=gslot)
            DMA("sp", w2s[f][r][g * 256:(g + 1) * 256, :].rearrange("(j p) c -> p j c", p=128),
                wst[i][:, :].rearrange("p (j c) -> p j c", j=2), [B_wst[i]], [B_w2s[f][r]], B_wst[i])

    def wcols(g):
        if g < 4:
            return g * 256
        if g < 6:
            return 1056 + (g - 4) * 256
        if g < 8:
            return 1568 + (g - 6) * 256
        return 2080 + (g - 8) * 256

    def conv_win():
        for g in range(10):
            i = cvt_i[0] % 2
            cvt_i[0] += 1
            c0 = wcols(g)
            cast_load(wst[i][:, :].rearrange("p (k c) -> p k c", k=8),
                      w_in[:, c0:c0 + 256].rearrange("(k p) c -> p k c", p=128),
                      lambda fl: fl.rearrange("p (k c) -> p k c", k=8), [B_wst[i]])
            DMA("sp", wins[g], wst[i][:, :], [B_wst[i]], [B_wins], B_wst[i])

    def load_W1(f):
        for k in range(8):
            for (c0, cn) in ((0, 2048), (2048, 2048), (4096, 1536)):
                cast_load(W1[:, k, c0:c0 + cn], ff_in[f][k * 128:(k + 1) * 128, c0:c0 + cn],
                          (lambda cn: lambda fl: fl[:, 0:cn])(cn), [PW(B_W1)])

    wglr = sb("wglr", [128, 8, 32], BF16)

    xt = [sb("xt%d" % i, [128, 2, D]) for i in range(2)]
    B_xt = [Buf("xt%d" % i) for i in range(2)]
    xn = takeA([128, 2, D], BF16)
    B_xn = Buf("xn")
    yT = [takeA([128, 8, T], BF16) for i in range(2)]
    B_yT = [Buf("yT%d" % i) for i in range(2)]
    st = [sb("st%d" % i, [128, 8]) for i in range(4)]
    B_st = [Buf("st%d" % i) for i in range(4)]
    st_i = [0]
    sg = [takeA([128, T]) for i in range(3)]
    B_sg = [Buf("sg%d" % i) for i in range(3)]
    hT = [takeA([128, T], BF16) for i in range(4)]
    B_hT = [Buf("hT%d" % i) for i in range(4)]
    GUB = [F[0][:, :], F[1][:, :], TP[1][:, :].bitcast(F32)]
    B_GUB = [B_F[0], B_F[1], B_TP[1]]
    wst = [takeA([128, 2048], BF16) for i in range(2)]
    B_wst = [Buf("wst%d" % i) for i in range(2)]
    wst_i = [0]
    cvt_i = [0]
    wstx = [G[:, i, :].bitcast(BF16) for i in range(3)]
    B_wstx = [Buf("wstx%d" % i) for i in range(3)]
    wpool = wst + wstx
    B_wpool = B_wst + B_wstx
    NWP = len(wpool)

    xt_i = [0]
    yT_i = [0]
    tp_i = [0]

    def rstd_of(ssap, n_el, eps, reads):
        i = st_i[0] % 4
        st_i[0] += 1
        w = ssap.shape[1]
        TS("dve", st[i][:, 0:w], ssap, 1.0 / n_el, eps, ALU.mult, ALU.add, reads, [B_st[i]])
        ACT(st[i][:, 0:w], st[i][:, 0:w], AF.Ln, [B_st[i]], [B_st[i]])
        ACT(st[i][:, 0:w], st[i][:, 0:w], AF.Exp, [B_st[i]], [B_st[i]], scale=-0.5)
        return st[i][:, 0:w], B_st[i]

    def norm_to_yT(xtile, B_x, n, r, yi, two_tp=False):
        i = st_i[0] % 4
        st_i[0] += 1
        ss, B_ss = st[i][:, 4:6], B_st[i]
        MS("pool", ss, 0.0, [B_ss])
        for s in range(2):
            ACT(tmp[:, :], xtile[:, s, :], AF.Square, [B_x, B_ss], [PW(B_tmp), B_ss], accum=ss[:, s:s + 1])
        rs, B_rs = rstd_of(ss, float(D), EPS, [B_ss])
        TS("dve", xn[:, 0, :], xtile[:, 0, :], rs[:, 0:1], None, ALU.mult, None, [B_x, B_rs], [PW(B_xn)])
        ACT(xn[:, 1, :], xtile[:, 1, :], AF.Identity, [B_x, B_rs], [PW(B_xn)], scale=rs[:, 1:2])
        for s in range(2):
            ti = s if two_tp else 0
            for k in range(8):
                S.op("pe", (lambda ti, k, s: lambda h: h.transpose(out=TP[ti][:, k * 128:(k + 1) * 128],
                                                                 in_=xn[:, s, k * 128:(k + 1) * 128], identity=identb[:]))(ti, k, s),
                     [B_xn, B_c], [B_TP[ti]])
            for k in range(8):
                if s == 0:
                    TS("dve", yT[yi][:, k, s * 128:(s + 1) * 128], TP[ti][:, k * 128:(k + 1) * 128],
                       AB[:, n, r, 0, k:k + 1], AB[:, n, r, 1, k:k + 1], ALU.mult, ALU.add, [B_TP[ti], B_mod], [PW(B_yT[yi])])
                else:
                    ACT(yT[yi][:, k, s * 128:(s + 1) * 128], TP[ti][:, k * 128:(k + 1) * 128], AF.Identity,
                        [B_TP[ti], B_mod], [PW(B_yT[yi])], scale=AB[:, n, r, 0, k:k + 1], bias=AB[:, n, r, 1, k:k + 1])
        return yT[yi], B_yT[yi]

    def ffn(y, B_y, f, r, xtile, B_x, inject=None, inject_at=9):
        FFv = [g_.rearrange("p (a t) -> p a t", a=2) for g_ in GUB]
        wcur = [None]

        def GU(j):
            fb = j % 3
            for a in range(2):
                for k in range(8):
                    MM(FFv[fb][:, a, :], W1[:, k, a * DFF + j * 128:a * DFF + (j + 1) * 128], y[:, k, :], k == 0, k == 7,
                       [B_W1, B_y], [B_GUB[fb]])
            ACT(sg[fb][:], FFv[fb][:, 0, :], AF.Silu, [B_GUB[fb]], [B_sg[fb]])
            TT("dve", hT[j % 4][:], sg[fb][:], FFv[fb][:, 1, :], ALU.mult, [B_sg[fb], B_GUB[fb]], [B_hT[j % 4]])

        def OUT(j):
            if j % 2 == 0:
                wi = wst_i[0] % NWP
                wst_i[0] += 1
                g = j // 2
                DMA("sp", wpool[wi][:, :].rearrange("p (j c) -> p j c", j=2),
                    w2s[f][r][g * 256:(g + 1) * 256, :].rearrange("(j p) c -> p j c", p=128), [B_w2s[f][r]], [B_wpool[wi]], B_wpool[wi])
                wcur[0] = wi
            wi = wcur[0]
            wv = wpool[wi][:, :].rearrange("p (j c) -> p j c", j=2)
            for s in range(2):
                for hf in range(2):
                    MM(F[2 + s * 2 + hf][:, :], hT[j % 4][:, s * 128:(s + 1) * 128], wv[:, j % 2, hf * 512:(hf + 1) * 512],
                       j == 0, j == NFF - 1, [B_hT[j % 4], B_wpool[wi]], [B_F[2 + s * 2 + hf]])

        GU(0)
        GU(1)
        for j in range(NFF):
            if j + 2 < NFF:
                GU(j + 2)
            OUT(j)
            if inject is not None:
                if isinstance(inject, dict):
                    if j in inject:
                        inject[j]()
                elif j == inject_at:
                    inject()
        for s in range(2):
            for hf in range(2):
                TT("dve", xtile[:, s, hf * 512:(hf + 1) * 512], F[2 + s * 2 + hf][:, :], xtile[:, s, hf * 512:(hf + 1) * 512], ALU.add,
                   [B_F[2 + s * 2 + hf], B_x], [PW(B_x)])

    zT = takeA([128, 12, T], BF16)
    B_zT = Buf("zT")
    bmT = takeA([128, 4, T], BF16)
    B_bmT = Buf("bmT")
    tmp = bmT.rearrange("p c t -> p (c t)")
    B_tmp = B_bmT
    glro = takeA([32, T])
    B_glro = Buf("glro")
    us = takeA([128, 512])
    gt = takeA([128, 512])
    gu = takeA([128, 512])
    gv = takeA([128, 512])
    B_us, B_gt, B_gu, B_gv = Buf("us"), Buf("gt"), Buf("gu"), Buf("gv")
    vsn = takeA([128, 512], BF16)
    B_vsn = Buf("vsn")
    bmk = takeA([128, 512], BF16)
    B_bmk = Buf("bmk")

    def gelu2(dst, B_dst, src_ps, B_src):
        CP("act", us[:], src_ps, [B_src], [B_us])
        TT("dve", gt[:], us[:], us[:], ALU.mult, [B_us], [B_gt])
        TS("dve", gt[:], gt[:], 0.044715, 1.0, ALU.mult, ALU.add, [B_gt], [B_gt])
        TT("dve", gt[:], gt[:], us[:], ALU.mult, [B_gt, B_us], [B_gt])
        ACT(gt[:], gt[:], AF.Tanh, [B_gt], [B_gt], scale=0.7978845608028654)
        STT("dve", dst, gt[:], 1.0, us[:], ALU.add, ALU.mult, [B_gt, B_us], [B_dst])

    def win_groups(y2, B_y2, b, tok0, is_ctx):
        FFv = [F[0][:, :].rearrange("p (a t) -> p a t", a=2), F[1][:, :].rearrange("p (a t) -> p a t", a=2)]

        def group(g):
            wi = wst_i[0] % NWP
            wst_i[0] += 1
            DMA("sp", wpool[wi][:, :], wins[g], [B_wins], [B_wpool[wi]], B_wpool[wi])
            wv = wpool[wi][:, :].rearrange("p (k c) -> p k c", k=8)
            if g < 6:
                fb = g % 2
                for cc in range(2):
                    for k in range(8):
                        MM(FFv[fb][:, cc, :], wv[:, k, cc * 128:(cc + 1) * 128], y2[:, k, :], k == 0, k == 7,
                           [B_wpool[wi], B_y2], [B_F[fb]])
                if g < 4:
                    CP("act" if g % 2 == 0 else "dve", zT[:, 2 * g:2 * g + 2, :], FFv[fb][:, :, :], [B_F[fb]], [PW(B_zT)])
                else:
                    ACT(zT[:, 2 * g:2 * g + 2, :], FFv[fb][:, :, :], AF.Silu, [B_F[fb]], [PW(B_zT)])
            else:
                which = (g - 6) // 2
                hf = (g - 6) % 2
                for s in range(2):
                    fi = 2 + which * 2 + s
                    for k in range(8):
                        MM(F[fi][:, hf * 256:(hf + 1) * 256], y2[:, k, s * 128:(s + 1) * 128], wv[:, k, :], k == 0, k == 7,
                           [B_wpool[wi], B_y2], [B_F[fi]])

        def glr_part():
            for k in range(8):
                MM(F[0][0:32, 0:T], wglr[:, k, :], y2[:, k, :], k == 0, k == 7, [B_par, B_y2], [B_F[0]])
            CP("act", glro[:], F[0][0:32, 0:T], [B_F[0]], [B_glro])

        gu2 = xn[:, 0, :].bitcast(F32)
        vsn2 = xn[:, 1, 0:512]
        GUS = [(gu, B_gu, vsn, B_vsn), (gu2, B_xn, vsn2, B_xn)]

        def cm1(s):
            gu, B_gu, vsn, B_vsn = GUS[s]
            gelu2(gu[:], B_gu, F[2 + s][:, :], B_F[2 + s])
            gelu2(gv[:], B_gv, F[4 + s][:, :], B_F[4 + s])
            i = st_i[0] % 4
            st_i[0] += 1
            ss, B_ss = st[i][:, 4:5], B_st[i]
            MS("pool", ss, 0.0, [B_ss])
            ACT(gt[:], gv[:], AF.Square, [B_gv, B_ss], [B_gt, B_ss], accum=ss)
            rs, B_rs = rstd_of(ss, 512.0, 4.0 * EPS, [B_ss])
            STT("dve", vsn[:], gv[:], rs[:, 0:1], CMG[:], ALU.mult, ALU.mult, [B_gv, B_rs, B_mod], [PW(B_vsn)] if s else [B_vsn])

        def cm2(s):
            gu, B_gu, vsn, B_vsn = GUS[s]
            fb = s % 2
            for hh in range(8):
                MM(F[fb][:, hh * 64:(hh + 1) * 64], wsT[:, hh, :], vsn[:, hh * 64:(hh + 1) * 64], True, True,
                   [B_mod, B_vsn], [B_F[fb]])
            for hh in range(8):
                STT("dve", bmk[:, hh * 64:(hh + 1) * 64], F[fb][:, hh * 64:(hh + 1) * 64], bsT[:, hh:hh + 1],
                    gu[:, hh * 64:(hh + 1) * 64], ALU.add, ALU.mult, [B_F[fb], B_mod, B_gu], [B_bmk])
            ti = 0
            for cc in range(4):
                S.op("pe", (lambda ti, cc: lambda h: h.transpose(out=TP[ti][:, cc * 128:(cc + 1) * 128],
                                                              in_=bmk[:, cc * 128:(cc + 1) * 128], identity=identb[:]))(ti, cc),
                     [B_bmk, B_c], [B_TP[ti]])
            CP("act", bmT[:, :, s * 128:(s + 1) * 128], TP[ti][:, 0:512].rearrange("p (c t) -> p c t", c=4), [B_TP[ti]], [B_bmT])

        if is_ctx:
            for g in range(4):
                group(g)
            glr_part()
            DMA("sp", glrcs[b][:, :], glro[:], [B_glro], [B_glrcs[b]], B_glro)
            DMA("sp", zcs[b].rearrange("c p t -> p c t"), zT[:, 0:8, :], [B_zT], [B_zcs[b]], B_zT)
            return
        for g in (6, 7, 8, 9):
            group(g)
        S.cap = []
        cm1(0)
        cm1(1)
        thunks, S.cap = S.cap, None
        per = (len(thunks) + 5) // 6
        for g in range(6):
            group(g)
            for th in thunks[g * per:(g + 1) * per]:
                th()
        for th in thunks[6 * per:]:
            th()
        glr_part()
        DMA("sp", glrs[b][:, tok0:tok0 + T], glro[:], [B_glro], [B_glrs[b]], B_glro)
        DMA("sp", zs[b][:, :, tok0:tok0 + T].rearrange("c p t -> p c t"), zT[:, :, :], [B_zT], [B_zs[b]], B_zT)
        cm2(0)
        cm2(1)
        DMA("sp", bms[b][:, :, tok0:tok0 + T].rearrange("c p t -> p c t"), bmT[:, :, :], [B_bmT], [B_bms[b]], B_bmT)

    def load_x(src_ap, rbufs, xi):
        DMA("sp", xt[xi][:, :, :], src_ap.rearrange("(s p) d -> p s d", p=128), rbufs, [B_xt[xi]], B_xt[xi])

    def run_phaseA(tiles):
        n = len(tiles)
        load_x(tiles[0][0], [], 0)
        norm_to_yT(xt[0], B_xt[0], 0, tiles[0][3], 0)
        for i, (src, b, tok0, r, is_ctx) in enumerate(tiles):
            xi = i % 2

            def inj(i=i):
                if i + 1 < n:
                    load_x(tiles[i + 1][0], [], (i + 1) % 2)
                    norm_to_yT(xt[(i + 1) % 2], B_xt[(i + 1) % 2], 0, tiles[i + 1][3], (i + 1) % 2)
            ffn(yT[xi], B_yT[xi], 0, r, xt[xi], B_xt[xi], inject=inj)
            if not is_ctx:
                DMA("sp", x1s[b][tok0:tok0 + T, :].rearrange("(s p) d -> p s d", p=128), xt[xi][:, :, :], [B_xt[xi]], [B_x1s[b]], B_xt[xi])
            y2, B_y2 = norm_to_yT(xt[xi], B_xt[xi], 1, r, xi, two_tp=True)
            win_groups(y2, B_y2, b, tok0, is_ctx)

    def run_phaseC2(tiles):
        n = len(tiles)
        b0, t0 = tiles[0]
        load_x(x2s[b0][t0:t0 + T, :], [B_x2s[b0]], 0)
        norm_to_yT(xt[0], B_xt[0], 2, b0, 0)
        fin_rs = {}

        def fin_act(i):
            xi = i % 2
            k = st_i[0] % 4
            st_i[0] += 1
            ss, B_ss = st[k][:, 4:6], B_st[k]
            MS("pool", ss, 0.0, [B_ss])
            for s in range(2):
                ACT(tmp[:, :], xt[xi][:, s, :], AF.Square, [B_xt[xi], B_ss], [PW(B_tmp), B_ss], accum=ss[:, s:s + 1])
            fin_rs[i] = rstd_of(ss, float(D), EPS, [B_ss])

        def fin_dve(i):
            xi = i % 2
            b, tok0 = tiles[i]
            rs, B_rs = fin_rs[i]
            for s in range(2):
                STT("dve", xt[xi][:, s, :], xt[xi][:, s, :], rs[:, s:s + 1], FG[:, :], ALU.mult, ALU.mult,
                    [B_xt[xi], B_rs, B_mod], [PW(B_xt[xi])])
            o = DMA("sp", out[b][tok0:tok0 + T, :].rearrange("(s p) d -> p s d", p=128), xt[xi][:, :, :], [B_xt[xi]], [], B_xt[xi])
            S.final.append(o)

        for i, (b, tok0) in enumerate(tiles):
            xi = i % 2
            inj = {}
            if i > 0:
                inj[2] = (lambda i=i: fin_act(i - 1))
                inj[6] = (lambda i=i: fin_dve(i - 1))

            def inj_norm(i=i):
                if i + 1 < n:
                    bn, tn = tiles[i + 1]
                    load_x(x2s[bn][tn:tn + T, :], [B_x2s[bn]], (i + 1) % 2)
                    norm_to_yT(xt[(i + 1) % 2], B_xt[(i + 1) % 2], 2, bn, (i + 1) % 2)
            inj[10] = inj_norm
            ffn(yT[xi], B_yT[xi], 1, b, xt[xi], B_xt[xi], inject=inj)
        fin_act(n - 1)
        fin_dve(n - 1)

    zflat = takeB([128, 8 * (ROWS + 2) * 66], BF16)
    zin = [zflat.rearrange("p (c r q) -> p c r q", c=8, r=ROWS + 2)]
    B_zin = [Buf("zin0")]
    zinc = zflat[:, 0:8 * (CTX + 2)].rearrange("p (c r q) -> p c r q", c=8, r=1)
    B_zinc = B_zin[0]
    zraw = takeB([128, 8, (ROWS + 2) * 64], BF16)
    B_zraw = Buf("zraw")
    qkvT = takeB([128, 8, T], BF16)
    B_qkvT = Buf("qkvT")
    glrT = [takeB([33, T]) for i in range(2)]
    B_glrT = [Buf("glrT%d" % i) for i in range(2)]
    ee = takeB([128, 512])
    gneg = ee
    Ep = takeB([128, 4, 128])
    Em = takeB([128, 4, 128])
    B_ee, B_Ep, B_Em = Buf("ee"), Buf("Ep"), Buf("Em")
    B_gneg = B_ee
    qk = [sb("qk%d" % i, [128, 4, 2, T], BF16) for i in range(2)]
    B_qk = [Buf("qk%d" % i) for i in range(2)]
    tokm = [takeB([128, 1024], BF16)]
    B_tokm = [Buf("tokm0")]
    zin_i = [0]
    tok_i = [0]
    qk_i = [0]
    DG = sb("DG", [128, 8, 9, 128], BF16)
    for cch in range(8):
        for tap in range(9):
            TS("dve", DG[:, cch, tap, :], identf[:], cw[:, cch, tap:tap + 1], None, ALU.mult, None, [B_c, B_mod], [PW(B_mod)])

    NA = (NCH + 1) * 256
    arf = arena[:, 0:4 * NA].bitcast(F32)
    ARR = arf[:, 0:NA].rearrange("p (n q v) -> p n q v", q=2, v=128)
    BRR = arf[:, NA:2 * NA].rearrange("p (n q v) -> p n q v", q=2, v=128)
    wo_off = 4 * NA
    Wout = arena[:, wo_off:wo_off + 8 * D].rearrange("p (k c) -> p k c", k=8)
    assert wo_off + 8 * D <= 8 * 2 * DFF
    B_ARR, B_BRR, B_Wout = Buf("ARR"), Buf("BRR"), Buf("Wout")
    ARRc = takeB([128, 3, 2, 128])
    BRRc = takeB([128, 3, 2, 128])
    B_ARRc, B_BRRc = Buf("ARRc"), Buf("BRRc")
    DFt = sb("DF", [128, NCH, 2])
    DBt = sb("DB", [128, NCH, 2])
    DFc = sb("DFc", [128, 2, 2])
    DBc = sb("DBc", [128, 2, 2])
    B_DF, B_DB, B_DFc, B_DBc = Buf("DF"), Buf("DB"), Buf("DFc"), Buf("DBc")

    def geomB(t, is_ctx):
        if is_ctx:
            return 1, CTX, 1, 0, CTX
        tok0 = t * T
        lo = tok0 - 64 if t > 0 else tok0
        hi = tok0 + T + 64 if t < NT - 1 else tok0 + T
        r_lo = 0 if t > 0 else 1
        return ROWS, 64, r_lo, lo, hi

    def B_load(b, t, is_ctx):
        R, C, r_lo, lo, hi = geomB(t, is_ctx)
        gi = t % 2
        if is_ctx:
            DMA("sp", zraw[:, :, 0:CTX], zcs[b].rearrange("c p t -> p c t"), [B_zcs[b]], [B_zraw], B_zraw)
            DMA("sp", glrT[gi][0:32, :], glrcs[b][:, :], [B_glrcs[b]], [B_glrT[gi]], B_glrT[gi])
        else:
            DMA("sp", zraw[:, :, 0:hi - lo], zs[b][0:8, :, lo:hi].rearrange("c p t -> p c t"), [B_zs[b]], [B_zraw], B_zraw)
            DMA("sp", glrT[gi][0:32, :], glrs[b][:, t * T:(t + 1) * T], [B_glrs[b]], [B_glrT[gi]], B_glrT[gi])

    def B_pad(b, t, is_ctx):
        R, C, r_lo, lo, hi = geomB(t, is_ctx)
        zi, B_z = (zinc, B_zinc) if is_ctx else (zin[0], B_zin[0])
        if is_ctx:
            for cch in range(8):
                CP("pool", zi[:, cch, 0:1, 1:C + 1], zraw[:, cch, 0:C].rearrange("p (r q) -> p r q", q=C), [B_zraw], [PW(B_z)])
            return
        nr = (hi - lo) // 64
        if t == 0:
            MS("pool", zi[:, :, 0, :], 0.0, [PW(B_z)])
        if t == NT - 1:
            MS("pool", zi[:, :, R + 1, :], 0.0, [PW(B_z)])
        for cch in range(8):
            CP("pool", zi[:, cch, r_lo:r_lo + nr, 1:65], zraw[:, cch, 0:nr * 64].rearrange("p (r q) -> p r q", q=64),
               [B_zraw], [PW(B_z)])

    def B_conv(b, t, is_ctx):
        R, C, r_lo, lo, hi = geomB(t, is_ctx)
        zi, B_z = (zinc, B_zinc) if is_ctx else (zin[0], B_zin[0])
        taps = [(1, 0), (1, 1), (1, 2)] if is_ctx else [(dr, dc) for dr in range(3) for dc in range(3)]
        for g in range(4):
            for cc in range(2):
                cch = 2 * g + cc
                o_ = F[2 + g][:, cc * 256:(cc + 1) * 256].rearrange("p (r q) -> p r q", q=C)
                for ti_, (dr, dc) in enumerate(taps):
                    view = zi[:, cch, 0:1, dc:dc + C] if is_ctx else zi[:, cch, dr:dr + R, dc:dc + C]
                    MM(o_, DG[:, cch, dr * 3 + dc, :], view, ti_ == 0, ti_ == len(taps) - 1, [B_z, B_mod], [B_F[2 + g]])
            ACT(qkvT[:, 2 * g:2 * g + 2, :], F[2 + g][:, :].rearrange("p (a t) -> p a t", a=2), AF.Silu, [B_F[2 + g]], [PW(B_qkvT)])

    def B_gates(b, t, is_ctx):
        gi = t % 2
        qi = qk_i[0] % 2
        qk_i[0] += 1

        def front(s):
            n = 2 * t + s
            sl = slice(s * 128, (s + 1) * 128)
            MM(F[0][:, :], glrT[gi][0:33, sl], Wg[0:33, :], True, True, [B_glrT[gi], B_par], [B_F[0]])
            ACT(ee[:], F[0][:, :], AF.Exp, [B_F[0]], [B_ee], scale=-1.0)
            ACT(gneg[:], ee[:], AF.Ln, [B_ee], [B_gneg], bias=1.0)
            for cc in range(4):
                MM(F[1][:, cc * 128:(cc + 1) * 128], gneg[:, cc * 128:(cc + 1) * 128], (trif if cc < 2 else trib)[:], True, True,
                   [B_gneg, B_c], [B_F[1]])
            F1v = F[1][:, :].rearrange("p (c t) -> p c t", c=4)
            ACT(Ep[:], F1v, AF.Exp, [B_F[1]], [B_Ep])
            ACT(Em[:], F1v, AF.Exp, [B_F[1]], [B_Em], scale=-1.0)
            STT("dve", qk[qi][:, 0, :, sl], qkvT[:, 0:2, sl], 0.125, Ep[:, 0:2, :], ALU.mult, ALU.mult, [B_qkvT, B_Ep], [PW(B_qk[qi])])
            STT("dve", qk[qi][:, 1, :, sl], qkvT[:, 0:2, sl], 0.125, Ep[:, 2:4, :], ALU.mult, ALU.mult, [B_qkvT, B_Ep], [PW(B_qk[qi])])
            TT("dve", qk[qi][:, 2, :, sl], qkvT[:, 2:4, sl], Em[:, 0:2, :], ALU.mult, [B_qkvT, B_Em], [PW(B_qk[qi])])
            TT("dve", qk[qi][:, 3, :, sl], qkvT[:, 2:4, sl], Em[:, 2:4, :], ALU.mult, [B_qkvT, B_Em], [PW(B_qk[qi])])
            if is_ctx:
                CP("dve", DFc[:, n, :], Ep[:, 0:2, 127], [B_Ep], [PW(B_DFc)])
                CP("dve", DBc[:, n, :], Ep[:, 2:4, 0], [B_Ep], [PW(B_DBc)])
            else:
                CP("dve", DFt[:, n, :], Ep[:, 0:2, 127], [B_Ep], [PW(B_DF)])
                CP("dve", DBt[:, n, :], Ep[:, 2:4, 0], [B_Ep], [PW(B_DB)])

        def back(s):
            n = 2 * t + s
            sl = slice(s * 128, (s + 1) * 128)
            ti = tp_i[0] % 2
            tp_i[0] += 1
            srcs = [qk[qi][:, 2, 0, sl], qk[qi][:, 2, 1, sl], qk[qi][:, 3, 0, sl], qk[qi][:, 3, 1, sl]] + \
                   [qkvT[:, 4 + cc, sl] for cc in range(4)]
            for cc, sap in enumerate(srcs):
                S.op("pe", (lambda ti, cc, sap: lambda h: h.transpose(out=TP[ti][:, cc * 128:(cc + 1) * 128], in_=sap,
                                                                   identity=identb[:]))(ti, cc, sap),
                     [B_qk[qi], B_qkvT, B_c], [B_TP[ti]])
            ki = 0
            CP("act", tokm[ki][:, :], TP[ti][:, :], [B_TP[ti]], [B_tokm[ki]])
            if not is_ctx:
                DMA("sp", vts[b][n * 128:(n + 1) * 128, :], tokm[ki][:, 512:1024], [B_tokm[ki]], [B_vts[b]], B_tokm[ki])
            for d in range(2):
                for p in range(2):
                    MM(F[2 + d][:, p * 256:(p + 1) * 256], tokm[ki][:, d * 256 + p * 128:d * 256 + (p + 1) * 128],
                       tokm[ki][:, 512 + p * 256:512 + (p + 1) * 256], True, True, [B_tokm[ki]], [B_F[2 + d]])
            A_, B_A = (ARRc, B_ARRc) if is_ctx else (ARR, B_ARR)
            Bk, B_B = (BRRc, B_BRRc) if is_ctx else (BRR, B_BRR)
            for d in range(2):
                dst, B_dst, idx = (A_, B_A, n + 1) if d == 0 else (Bk, B_B, n)
                Fv = F[2 + d][:, :].rearrange("p (q x) -> p q x", q=2)
                e_ = "act" if d == 0 else "dve"
                CP(e_, dst[0:64, idx, :, :], Fv[0:64, :, 0:128], [B_F[2 + d]], [PW(B_dst)])
                CP(e_, dst[64:128, idx, :, :], Fv[64:128, :, 128:256], [B_F[2 + d]], [PW(B_dst)])

        front(0)
        front(1)
        back(0)
        back(1)
        if not is_ctx:
            DMA("sp", qks[b][t], qk[qi][:, :, :, :].rearrange("p a c t -> p (a c t)"), [B_qk[qi]], [B_qks[b]], B_qk[qi])

    def scan(A_, B_A, Bk, B_B, DFx, B_DFx, DBx, B_DBx, nch):
        for n in range(nch):
            TT("dve", A_[:, n + 1, :, :], A_[:, n + 1, :, :], A_[:, n, :, :], ALU.add, [B_A], [B_A])
            for p in range(2):
                TS("dve", A_[:, n + 1, p, :], A_[:, n + 1, p, :], DFx[:, n, p:p + 1], None, ALU.mult, None, [B_A, B_DFx], [B_A])
        for n in range(nch - 1, -1, -1):
            TT("dve", Bk[:, n, :, :], Bk[:, n, :, :], Bk[:, n + 1, :, :], ALU.add, [B_B], [B_B])
            for p in range(2):
                ACT(Bk[:, n, p, :], Bk[:, n, p, :], AF.Identity, [B_B, B_DBx], [B_B], scale=DBx[:, n, p:p + 1])

    vt = [takeC([128, 2, 512], BF16) for i in range(2)]
    B_vt = [Buf("vt%d" % i) for i in range(2)]
    zo = [takeC([128, 4, T], BF16) for i in range(2)]
    B_zo = [Buf("zo%d" % i) for i in range(2)]
    bmi = [takeC([128, 4, T], BF16) for i in range(2)]
    B_bmi = [Buf("bmi%d" % i) for i in range(2)]
    t1 = takeC([128, 4, 128], BF16)
    t2 = takeC([128, 4, 128], BF16)
    B_t1, B_t2 = Buf("t1"), Buf("t2")
    sq = takeC([128, 512], BF16)
    B_sq = Buf("sq")
    msb = takeC([128, 512])
    B_msb = Buf("msb")
    a1 = msb
    B_a1 = B_msb
    aT = takeC([128, 4, 128], BF16)
    B_aT = Buf("aT")
    SFn = takeC([128, 2, 128], BF16)
    SBn = takeC([128, 2, 128], BF16)
    B_SFn, B_SBn = Buf("SFn"), Buf("SBn")
    c1_i = [0]

    t1b = takeC([128, 4, 128], BF16)
    t2b = takeC([128, 4, 128], BF16)
    B_t1b, B_t2b = Buf("t1b"), Buf("t2b")
    OB = [F[2][:, :], TP[0][:, :].bitcast(F32)]
    B_OB = [B_F[2], B_TP[0]]
    T12 = [(t1, B_t1, t2, B_t2), (t1b, B_t1b, t2b, B_t2b)]

    def C1_load(b, t):
        ci = t % 2
        tok0 = t * T
        DMA("sp", qk[ci][:, :, :, :].rearrange("p a c t -> p (a c t)"), qks[b][t], [B_qks[b]], [B_qk[ci]], B_qk[ci])
        DMA("sp", vt[ci][:, :, :], vts[b][tok0:tok0 + T, :].rearrange("(s p) c -> p s c", p=128), [B_vts[b]], [B_vt[ci]], B_vt[ci])
        DMA("sp", zo[ci][:, :, :], zs[b][8:12, :, tok0:tok0 + T].rearrange("c p t -> p c t"), [B_zs[b]], [B_zo[ci]], B_zo[ci])
        DMA("sp", bmi[ci][:, :, :], bms[b][:, :, tok0:tok0 + T].rearrange("c p t -> p c t"), [B_bms[b]], [B_bmi[ci]], B_bmi[ci])
        DMA("sp", xt[ci][:, :, :], x1s[b][tok0:tok0 + T, :].rearrange("(s p) d -> p s d", p=128), [B_x1s[b]], [B_xt[ci]], B_xt[ci])

    def C1_compute(b, t):
        ci = t % 2
        xi = ci
        tok0 = t * T
        q_ = qk[ci]

        def stage1(s):
            n = 2 * t + s
            sl = slice(s * 128, (s + 1) * 128)
            ta, B_ta, tb, B_tb = T12[s]
            CP("act", SFn[:, :, :], ARR[:, n, :, :], [B_ARR], [B_SFn])
            CP("act", SBn[:, :, :], BRR[:, n + 1, :, :], [B_BRR], [B_SBn])
            for half in range(2):
                pr = slice(half * 64, half * 64 + 64)
                for p in range(2):
                    MM(F[half][:, p * 128:(p + 1) * 128], q_[pr, 2, p, sl], q_[pr, 0, p, sl], True, True, [B_qk[ci]], [B_F[half]])
                    MM(F[half][:, (2 + p) * 128:(3 + p) * 128], q_[pr, 3, p, sl], q_[pr, 1, p, sl], True, True, [B_qk[ci]], [B_F[half]])
            TT("dve", ta[:, :, :], F[0][:, :].rearrange("p (h i) -> p h i", h=4), mf4[:, :, :], ALU.mult, [B_F[0], B_c], [B_ta])
            TT("dve", tb[:, :, :], F[1][:, :].rearrange("p (h i) -> p h i", h=4), mf4[:, :, :], ALU.mult, [B_F[1], B_c], [B_tb])
            for hh in range(4):
                p, pr = hh // 2, slice((hh % 2) * 64, (hh % 2) * 64 + 64)
                o_ = OB[s][:, hh * 128:(hh + 1) * 128]
                tm = ta if hh % 2 == 0 else tb
                MM(o_, vt[ci][:, s, hh * 128:(hh + 1) * 128], tm[:, p, :], True, False, [B_vt[ci], B_ta, B_tb], [B_OB[s]])
                MM(o_, vt[ci][:, s, hh * 128:(hh + 1) * 128], tm[:, 2 + p, :], False, False, [B_vt[ci], B_ta, B_tb], [B_OB[s]])
                MM(o_, SFn[pr, p, :], q_[pr, 0, p, sl], False, False, [B_SFn, B_qk[ci]], [B_OB[s]])
                MM(o_, SBn[pr, p, :], q_[pr, 1, p, sl], False, True, [B_SBn, B_qk[ci]], [B_OB[s]])

        def stage2(s):
            sl = slice(s * 128, (s + 1) * 128)
            ACT(sq[:], OB[s], AF.Square, [B_OB[s]], [B_sq])
            MM(F[3][:, :], onesb[:], sq[:], True, True, [B_c, B_sq], [B_F[3]])
            TS("dve", msb[:], F[3][:, :], 1.0 / 128.0, EPS, ALU.mult, ALU.add, [B_F[3]], [B_msb])
            ACT(msb[:], msb[:], AF.Ln, [B_msb], [B_msb])
            ACT(msb[:], msb[:], AF.Exp, [B_msb], [B_msb], scale=-0.5)
            TT("dve", a1[:], OB[s], msb[:], ALU.mult, [B_OB[s], B_msb], [B_a1])
            STT("dve", aT[:, :, :], a1[:].rearrange("p (h i) -> p h i", h=4), glag[:, 0:1], zo[ci][:, :, sl], ALU.mult, ALU.mult,
                [B_a1, B_par, B_zo[ci]], [B_aT])
            for hf in range(2):
                for cc in range(4):
                    MM(F[4 + hf][:, :], aT[:, cc, :], Wout[:, cc, hf * 512:(hf + 1) * 512], cc == 0, False, [B_aT, B_Wout], [B_F[4 + hf]])
                for cc in range(4):
                    MM(F[4 + hf][:, :], bmi[ci][:, cc, sl], Wout[:, 4 + cc, hf * 512:(hf + 1) * 512], False, cc == 3,
                       [B_bmi[ci], B_Wout], [B_F[4 + hf]])
            for hf in range(2):
                TT("dve", xt[xi][:, s, hf * 512:(hf + 1) * 512], F[4 + hf][:, :], xt[xi][:, s, hf * 512:(hf + 1) * 512], ALU.add,
                   [B_F[4 + hf], B_xt[xi]], [PW(B_xt[xi])])

        stage1(0)
        stage1(1)
        stage2(0)
        stage2(1)
        DMA("sp", x2s[b][tok0:tok0 + T, :].rearrange("(s p) d -> p s d", p=128), xt[xi][:, :, :], [B_xt[xi]], [B_x2s[b]], B_xt[xi])

    def finish():
        with nc.allow_low_precision("bf16 matmul operands, fp32 accumulation"), nc.allow_non_contiguous_dma("layout"):
            S.emit()
        print("ops", {e: len(v) for e, v in S.ops.items()}, "dma sems", len(S.dma_bufs))
        return nc

    print("carve", stA, stB, stC, stP, "sbuf_left", nc.sbuf_bytes_remaining)
    assert stA[0] <= YB and stB[0] <= YB and stP[0] <= 8 * 2 * DFF * 2, (stA, stB, stP)
    A_set = [B_xn] + B_yT + B_sg + B_hT + B_wst + [B_zT, B_bmT, B_glro, B_us, B_gt, B_gu, B_gv, B_vsn, B_bmk]
    B_set = [B_zin[0], B_zraw, B_qkvT] + B_glrT + [B_ee, B_Ep, B_Em, B_tokm[0], B_ARRc, B_BRRc]
    C_set = B_vt + B_zo + B_bmi + [B_t1, B_t2, B_t1b, B_t2b, B_sq, B_msb, B_aT, B_SFn, B_SBn]
    assert stC[0] <= YB
    make_FG(2)
    cast_load(wglr[:], w_in[:, 1024:1056].rearrange("(k p) c -> p k c", p=128),
              lambda fl: fl[:, 0:256].rearrange("p (k c) -> p k c", k=8), [B_par])
    fence(B_W1, B_wa + [B_par, B_mod, B_mrow])
    load_W1(0)
    for r in range(3):
        make_G(r, 2, r, 0.5)
    conv_w2(0, 2, 2)
    for b in range(NB):
        conv_w2(0, b, b)
    conv_win()
    if stage == 0:
        return finish()
    for i in range(3):
        fence(B_wstx[i], [B_G])
    tilesA = [(ctx[b], b, 0, 2, True) for b in range(NB)]
    tilesA += [(x[b][t * T:(t + 1) * T, :], b, t * T, b, False) for b in range(NB) for t in range(NT)]
    run_phaseA(tilesA)
    if stage == 2:
        return finish()
    fence(B_ARR, [B_W1])
    fence(B_BRR, [B_W1])
    fence(B_Wout, [B_W1])
    fence(B_G, B_wstx)
    for b in range(NB):
        make_G(b, 8, b, 0.5)
        conv_w2(1, b, b)
    for b in range(NB):
        make_G(b, 5, b, 1.0)
    for b in range(NB):
        for bb in B_set:
            fence(bb, A_set + C_set)
        for k in range(8):
            cast_load(Wout[:, k, :], w_out[k * 128:(k + 1) * 128, :], lambda fl: fl[:, 0:1024], [PW(B_Wout)], gslot=b)
        MS("pool", ARRc[:, 0, :, :], 0.0, [B_ARRc])
        MS("pool", BRRc[:, 2, :, :], 0.0, [B_BRRc])
        for i in range(2):
            MS("pool", glrT[i][:], 1.0, [B_glrT[i]])
        MS("pool", zflat[:, :], 0.0, [B_zin[0]])
        B_load(b, 0, True)
        B_pad(b, 0, True)
        B_conv(b, 0, True)
        B_gates(b, 0, True)
        MS("pool", zflat[:, :], 0.0, [B_zin[0]])
        scan(ARRc, B_ARRc, BRRc, B_BRRc, DFc, B_DFc, DBc, B_DBc, 2)
        CP("dve", ARR[:, 0, :, :], ARRc[:, 2, :, :], [B_ARRc], [B_ARR])
        CP("pool", BRR[:, NCH, :, :], BRRc[:, 0, :, :], [B_BRRc], [B_BRR])
        if stage == 2.3:
            return finish()
        B_load(b, 0, False)
        B_pad(b, 0, False)
        for t in range(NT):
            B_conv(b, t, False)
            if t + 1 < NT:
                B_load(b, t + 1, False)
                B_pad(b, t + 1, False)
            B_gates(b, t, False)
        if stage == 2.4:
            return finish()
        scan(ARR, B_ARR, BRR, B_BRR, DFt, B_DF, DBt, B_DB, NCH)
        if stage == 2.5:
            return finish()
        for bb in C_set:
            fence(bb, A_set + B_set)
        C1_load(b, 0)
        for t in range(NT):
            if t + 1 < NT:
                C1_load(b, t + 1)
            C1_compute(b, t)
    if stage == 3:
        return finish()
    B_W1b = Buf("W1b")
    fence(B_W1b, [B_ARR, B_BRR, B_Wout])
    B_W1.ws = []
    B_W1.open = False
    B_W1.readers = list(B_W1b.readers)
    load_W1(1)
    for bb in A_set:
        fence(bb, B_set + C_set)
    for i in range(3):
        fence(B_wstx[i], [B_G])
    run_phaseC2([(b, t * T) for b in range(NB) for t in range(NT)])

    return finish()


_CACHE = {}


def _core_inputs(inputs, b0, NB, SEQ):
    f = lambda a: np.ascontiguousarray(np.asarray(a, dtype=np.float32))
    g = np.concatenate([np.asarray(inputs["norm1_g"][0]).reshape(8, 128), np.asarray(inputs["norm2_g"][0]).reshape(8, 128),
                        np.asarray(inputs["norm3_g"][0]).reshape(8, 128)], axis=0)
    return {
        "x": f(inputs["x"][b0:b0 + NB, :SEQ]),
        "c": f(inputs["c"][b0:b0 + NB]),
        "ctx": f(inputs["ctx"][b0:b0 + NB]),
        "c_ctx": f(np.asarray(inputs["c_ctx"]).reshape(1, D)),
        "w_ada": f(inputs["w_ada"][0]),
        "b_ada": f(np.asarray(inputs["b_ada"][0]).reshape(72, 128)),
        "gains": f(g),
        "ff1_in": f(inputs["ff1_in"][0]),
        "ff2_in": f(inputs["ff2_in"][0]),
        "ff1_out": f(inputs["ff1_out"][0]),
        "ff2_out": f(inputs["ff2_out"][0]),
        "w_in": f(inputs["w_in"][0]),
        "conv_w": f(np.asarray(inputs["conv_w"][0]).reshape(9, D)),
        "w_gate_f": f(inputs["w_gate_f"][0]),
        "b_gate_f": f(np.asarray(inputs["b_gate_f"][0]).reshape(1, 256)),
        "w_gate_b": f(inputs["w_gate_b"][0]),
        "b_gate_b": f(np.asarray(inputs["b_gate_b"][0]).reshape(1, 256)),
        "gla_norm_g": f(np.asarray(inputs["gla_norm_g"][0]).reshape(128, 1)),
        "cmlp_norm_g": f(np.asarray(inputs["cmlp_norm_g"][0]).reshape(1, 512)),
        "w_s": f(inputs["w_s"][0]),
        "b_s": f(inputs["b_s"][0]),
        "w_out": f(inputs["w_out"][0]),
        "final_g": f(np.asarray(inputs["final_g"]).reshape(1, D)),
    }


def kernel(**inputs):
    x = np.asarray(inputs["x"])
    B, SEQ = x.shape[0], x.shape[1]
    ncores = 8
    NB = B // ncores
    key = (NB, SEQ)
    if key not in _CACHE:
        _CACHE[key] = build(NB, SEQ)
    nc = _CACHE[key]
    in_maps = [_core_inputs(inputs, i * NB, NB, SEQ) for i in range(ncores)]
    res = run_bass_kernel_spmd(nc, in_maps, core_ids=list(range(ncores)))
    return np.concatenate([np.asarray(r["out"]) for r in res.results], axis=0).astype(np.float32)
```

```python
import numpy as np
import concourse.bass as bass
import concourse.mybir as mybir
from concourse.bass_utils import run_bass_kernel_spmd

F32 = mybir.dt.float32
BF16 = mybir.dt.bfloat16
AF = mybir.ActivationFunctionType
ALU = mybir.AluOpType

D = 1024
DFF = 2816
NFF = 22
CTX = 256
T = 256
EPS = 1e-6


class Buf:
    __slots__ = ("name", "ws", "readers", "pre", "open", "sem", "cnt")

    def __init__(self, name):
        self.name = name
        self.ws = []
        self.readers = []
        self.pre = []
        self.open = False
        self.sem = None
        self.cnt = 0


class PW:
    __slots__ = ("buf",)

    def __init__(self, buf):
        self.buf = buf


class Op:
    __slots__ = ("eng", "fn", "deps", "is_dma", "sem", "val", "signals", "idx")


def fence(newbuf, oldbufs):
    for b in oldbufs:
        newbuf.readers.extend(b.ws)
        newbuf.readers.extend(b.readers)
        if b.open:
            newbuf.readers.extend(b.pre)
    newbuf.open = False


class Sched:
    ENGS = ("pe", "act", "dve", "pool", "sp")

    def __init__(self, nc):
        self.nc = nc
        self.ops = {e: [] for e in self.ENGS}
        self.n = 0
        self.final = []
        self.dma_bufs = []
        self.cap = None

    def _add(self, eng, fn, reads, writes, is_dma, dbuf=None):
        op = Op()
        op.eng = eng
        op.fn = fn
        op.is_dma = is_dma
        op.signals = is_dma
        op.sem = None
        op.val = None
        op.idx = self.n
        self.n += 1
        deps = {}
        for b in reads:
            for w in b.ws:
                deps[id(w)] = (w, True)
        for wb in writes:
            if isinstance(wb, PW):
                b = wb.buf
                if not b.open:
                    b.pre = b.ws + b.readers
                    b.ws = []
                    b.readers = []
                    b.open = True
                for d in b.pre:
                    if id(d) not in deps:
                        deps[id(d)] = (d, False)
            else:
                for d in wb.ws + wb.readers:
                    if id(d) not in deps:
                        deps[id(d)] = (d, False)
        keep = []
        for d, raw in deps.values():
            if d is op:
                continue
            if (not is_dma) and (not d.is_dma) and d.eng == eng:
                if eng == "pe" or (not raw and eng != "pool"):
                    continue
            keep.append(d)
            d.signals = True
        op.deps = keep
        for b in reads:
            b.readers.append(op)
            b.open = False
        for wb in writes:
            if isinstance(wb, PW):
                wb.buf.ws.append(op)
            else:
                wb.ws = [op]
                wb.readers = []
                wb.open = False
        if is_dma:
            if dbuf.sem is None:
                self.dma_bufs.append(dbuf)
                dbuf.sem = True
            dbuf.cnt += 1
            op.sem = dbuf
            op.val = 16 * dbuf.cnt
        self.ops[eng].append(op)
        return op

    def op(self, eng, fn, reads=(), writes=()):
        if self.cap is not None:
            r, w = list(reads), list(writes)
            self.cap.append(lambda: self._add(eng, fn, r, w, False))
            return None
        return self._add(eng, fn, list(reads), list(writes), False)

    def dma(self, eng, fn, reads=(), writes=(), dbuf=None, final=False):
        if self.cap is not None:
            r, w = list(reads), list(writes)

            def th():
                o = self._add(eng, fn, r, w, True, dbuf=dbuf)
                if final:
                    self.final.append(o)
            self.cap.append(th)
            return None
        o = self._add(eng, fn, list(reads), list(writes), True, dbuf=dbuf)
        if final:
            self.final.append(o)
        return o

    def emit(self):
        nc = self.nc
        esem = {e: nc.alloc_semaphore("s_" + e) for e in ("pe", "act", "dve", "pool")}
        for b in self.dma_bufs:
            b.sem = nc.alloc_semaphore("d_" + b.name)
        cnt = {e: 0 for e in esem}
        for e in self.ENGS:
            for o in self.ops[e]:
                if o.is_dma:
                    o.sem = o.sem.sem
                elif o.signals:
                    cnt[e] += 1
                    o.sem = esem[e]
                    o.val = cnt[e]
        final = self.final
        ops = self.ops

        def run(eng_name, h):
            waited = {}
            for o in ops[eng_name]:
                for d in o.deps:
                    key = d.sem.num
                    if waited.get(key, 0) >= d.val:
                        continue
                    h.wait_ge(d.sem, d.val)
                    waited[key] = d.val
                ins = o.fn(h)
                if o.signals:
                    ins.then_inc(o.sem, 16 if o.is_dma else 1)
            if eng_name == "sp":
                for d in final:
                    if waited.get(d.sem.num, 0) >= d.val:
                        continue
                    h.wait_ge(d.sem, d.val)
                    waited[d.sem.num] = d.val

        with nc.Block() as block:
            @block.tensor
            def _(h):
                run("pe", h)

            @block.scalar
            def _(h):
                run("act", h)

            @block.vector
            def _(h):
                run("dve", h)

            @block.gpsimd
            def _(h):
                run("pool", h)

            @block.sync
            def _(h):
                run("sp", h)


def build(NB, SEQ, dbg=False, stage=99):
    nc = bass.Bass("TRN2", target_bir_lowering=False)
    S = Sched(nc)
    NT = SEQ // T
    NCH = SEQ // 128
    ROWS = T // 64

    def din(name, shape):
        return nc.dram_tensor(name, list(shape), F32, kind="ExternalInput").ap()

    def dsc(name, shape, dt):
        return nc.dram_tensor(name, list(shape), dt, kind="Internal").ap()

    x = din("x", [NB, SEQ, D])
    c_in = din("c", [NB, D])
    ctx = din("ctx", [NB, CTX, D])
    c_ctx = din("c_ctx", [1, D])
    w_ada = din("w_ada", [D, 9 * D])
    b_ada = din("b_ada", [72, 128])
    gains = din("gains", [24, 128])
    ff_in = [din("ff1_in", [D, 2 * DFF]), din("ff2_in", [D, 2 * DFF])]
    ff_out = [din("ff1_out", [DFF, D]), din("ff2_out", [DFF, D])]
    w_in = din("w_in", [D, 2592])
    conv_w = din("conv_w", [9, D])
    w_gf = din("w_gate_f", [16, 256])
    b_gf = din("b_gate_f", [1, 256])
    w_gb = din("w_gate_b", [16, 256])
    b_gb = din("b_gate_b", [1, 256])
    gla_g = din("gla_norm_g", [128, 1])
    cm_g = din("cmlp_norm_g", [1, 512])
    w_s = din("w_s", [8, 128, 128])
    b_s = din("b_s", [8, 128])
    w_out = din("w_out", [D, D])
    final_g = din("final_g", [1, D])
    out = nc.dram_tensor("out", [NB, SEQ, D], F32, kind="ExternalOutput").ap()

    w2s = [[dsc("w2s0_%d" % r, [DFF, D], BF16) for r in range(3)], [dsc("w2s1_%d" % r, [DFF, D], BF16) for r in range(2)]]
    wins = dsc("wins", [10, 128, 8 * 256], BF16)
    x1s = dsc("x1s", [NB, SEQ, D], F32)
    x2s = dsc("x2s", [NB, SEQ, D], F32)
    zs = dsc("zs", [NB, 12, 128, SEQ], BF16)
    zcs = dsc("zcs", [NB, 8, 128, CTX], BF16)
    bms = dsc("bms", [NB, 4, 128, SEQ], BF16)
    glrs = dsc("glrs", [NB, 32, SEQ], F32)
    glrcs = dsc("glrcs", [NB, 32, CTX], F32)
    qks = dsc("qks", [NB, NT, 128, 2048], BF16)
    vts = dsc("vts", [NB, SEQ, 512], BF16)
    B_w2s = [[Buf("w2s0_%d" % r) for r in range(3)], [Buf("w2s1_%d" % r) for r in range(2)]]
    B_wins = Buf("wins")
    B_x1s = [Buf("x1s%d" % b) for b in range(NB)]
    B_x2s = [Buf("x2s%d" % b) for b in range(NB)]
    B_zs = [Buf("zs%d" % b) for b in range(NB)]
    B_zcs = [Buf("zcs%d" % b) for b in range(NB)]
    B_bms = [Buf("bms%d" % b) for b in range(NB)]
    B_glrs = [Buf("glrs%d" % b) for b in range(NB)]
    B_glrcs = [Buf("glrcs%d" % b) for b in range(NB)]
    B_qks = [Buf("qks%d" % b) for b in range(NB)]
    B_vts = [Buf("vts%d" % b) for b in range(NB)]

    def sb(name, shape, dt=F32):
        return nc.alloc_sbuf_tensor(name, list(shape), dt)

    def carver(base, start=0):
        state = [start]

        def take(shape, dt=F32):
            per = int(np.prod(shape[1:]))
            nb = per * (4 if dt == F32 else 2)
            o = state[0]
            state[0] += (nb + 31) // 32 * 32
            v = base[:, o // 2:(o + nb) // 2]
            if dt == F32:
                v = v.bitcast(F32)
            if len(shape) > 2:
                names = "abcd"[:len(shape) - 1]
                v = v.rearrange("p (%s) -> p %s" % (" ".join(names), " ".join(names)),
                                **{n: int(z) for n, z in zip(names, shape[1:])})
            return v[0:shape[0]] if shape[0] < 128 else v
        return take, state

    arena = sb("arena", [128, 8 * 2 * DFF], BF16)
    YB = 45568
    yreg = sb("yreg", [128, YB // 2], BF16)
    takeP, stP = carver(arena, 65536)
    takeA, stA = carver(yreg, 0)
    takeB, stB = carver(yreg, 0)
    takeC, stC = carver(yreg, 0)

    def MM(o, lhsT, rhs, st, sp, r, w):
        S.op("pe", lambda h: h.matmul(o, lhsT=lhsT, rhs=rhs, start=st, stop=sp), r, w)

    def ACT(o, i, func, r, w, scale=1.0, bias=None, accum=None):
        kw = {}
        if bias is not None:
            kw["bias"] = bias
        if accum is not None:
            kw["accum_out"] = accum
        S.op("act", lambda h: h.activation(out=o, in_=i, func=func, scale=scale, **kw), r, w)

    def TS(eng, o, i, s1, s2, op0, op1, r, w):
        if op1 is None:
            S.op(eng, lambda h: h.tensor_scalar(out=o, in0=i, scalar1=s1, scalar2=None, op0=op0), r, w)
        else:
            S.op(eng, lambda h: h.tensor_scalar(out=o, in0=i, scalar1=s1, scalar2=s2, op0=op0, op1=op1), r, w)

    def TT(eng, o, i0, i1, op, r, w):
        S.op(eng, lambda h: h.tensor_tensor(out=o, in0=i0, in1=i1, op=op), r, w)

    def STT(eng, o, i0, sc, i1, op0, op1, r, w):
        eng = "dve"
        S.op(eng, lambda h: h.scalar_tensor_tensor(out=o, in0=i0, scalar=sc, in1=i1, op0=op0, op1=op1), r, w)

    def CP(eng, o, i, r, w):
        if eng == "act":
            S.op("act", lambda h: h.copy(out=o, in_=i), r, w)
        else:
            S.op(eng, lambda h: h.tensor_copy(out=o, in_=i), r, w)

    def MS(eng, ap, val, w):
        S.op(eng, lambda h: h.memset(ap, val), [], w)

    def DMA(eng, o, i, r, w, dbuf, final=False):
        return S.dma(eng, lambda h: h.dma_start(out=o, in_=i), r, w, dbuf=dbuf, final=final)

    TP = [nc.alloc_psum_tensor("tp%d" % i, [128, 1024], BF16) for i in range(2)]
    B_TP = [Buf("tp%d" % i) for i in range(2)]
    F = [nc.alloc_psum_tensor("f%d" % i, [128, 512], F32) for i in range(6)]
    B_F = [Buf("f%d" % i) for i in range(6)]

    B_c = Buf("const")
    identf = sb("identf", [128, 128])
    identb = sb("identb", [128, 128], BF16)
    onesf = sb("onesf", [128, 128])
    onesb = sb("onesb", [128, 128], BF16)
    mf = sb("mf", [128, 128])
    mb = sb("mb", [128, 128])
    trif = sb("trif", [128, 128])
    trib = sb("trib", [128, 128])
    mf4 = sb("mf4", [128, 4, 128], BF16)
    mb4 = sb("mb4", [128, 4, 128], BF16)
    MS("pool", identf[:], 0.0, [B_c])
    S.op("pool", lambda h: h.affine_select(out=identf[:], in_=identf[:], pattern=[[-1, 128]], compare_op=ALU.not_equal,
                                           fill=1.0, base=0, channel_multiplier=1), [B_c], [B_c])
    CP("pool", identb[:], identf[:], [B_c], [B_c])
    MS("pool", onesf[:], 1.0, [B_c])
    MS("pool", onesb[:], 1.0, [B_c])
    MS("pool", mf[:], 1.0, [B_c])
    S.op("pool", lambda h: h.affine_select(out=mf[:], in_=mf[:], pattern=[[1, 128]], compare_op=ALU.is_ge,
                                           fill=0.0, base=0, channel_multiplier=-1), [B_c], [B_c])
    MS("pool", mb[:], 1.0, [B_c])
    S.op("pool", lambda h: h.affine_select(out=mb[:], in_=mb[:], pattern=[[-1, 128]], compare_op=ALU.is_ge,
                                           fill=0.0, base=0, channel_multiplier=1), [B_c], [B_c])
    TS("pool", trif[:], mf[:], -1.0 / 16.0, None, ALU.mult, None, [B_c], [B_c])
    TS("pool", trib[:], mb[:], -1.0 / 16.0, None, ALU.mult, None, [B_c], [B_c])
    for hh in range(4):
        CP("pool", mf4[:, hh, :], (mf if hh < 2 else mb)[:], [B_c], [B_c])

    W1 = arena[:, :].rearrange("p (k c) -> p k c", k=8)
    B_W1 = Buf("W1")

    B_par = Buf("params")
    crow = takeP([3, D])
    brow = takeP([72, 128])
    grow = takeP([24, 128])
    cwrow = takeP([9, D])
    bsrow = takeP([8, 128])
    fgrow = takeP([1, D])
    cmrow = takeP([1, 512])
    glag = sb("glag", [128, 1])
    Wg = sb("Wg", [33, 512])
    wssb = takeP([128, 8, 128])
    MS("pool", Wg[:], 0.0, [B_par])
    pl = []
    pl.append((crow[0:NB, :], c_in))
    pl.append((crow[2:3, :], c_ctx))
    pl.append((brow[:], b_ada))
    pl.append((grow[:], gains))
    pl.append((cwrow[:], conv_w))
    pl.append((bsrow[:], b_s))
    pl.append((fgrow[:], final_g))
    pl.append((cmrow[:], cm_g))
    pl.append((glag[:], gla_g))
    pl.append((Wg[0:16, 0:256], w_gf))
    pl.append((Wg[16:32, 256:512], w_gb))
    pl.append((Wg[32:33, 0:256], b_gf))
    pl.append((Wg[32:33, 256:512], b_gb))
    pl.append((wssb[:], w_s.rearrange("h i j -> i h j")))
    if NB < 2:
        MS("pool", crow[:], 0.0, [B_par])
    for (o, i) in pl:
        DMA("sp", o, i, [], [B_par], B_par)

    B_mod = Buf("mod")
    sc = crow
    scT = sb("scT", [128, 8, 4])
    mods = sb("mods", [128, 72, 3])
    bT = sb("bT", [128, 72])
    gT = sb("gT", [128, 24])
    AB = sb("AB", [128, 3, 3, 2, 8])
    cw = sb("cw", [128, 8, 9])
    bsT = sb("bsT", [128, 8])
    wsb = takeP([128, 8, 128], BF16)
    wsT = sb("wsT", [128, 8, 128], BF16)
    G = sb("G", [128, 3, D])
    FG = sb("FG", [128, D])
    B_G = Buf("G")
    CMG = sb("CMG", [128, 512])
    ACT(sc[:], crow[:], AF.Silu, [B_par], [B_mod])
    for k in range(8):
        MM(F[1][:, k * 4:k * 4 + 3], sc[0:3, k * 128:(k + 1) * 128], identf[0:3, 0:3], True, True, [B_mod, B_c], [B_F[1]])
    CP("dve", scT[:, :, 0:3], F[1][:, 0:32].rearrange("p (k r) -> p k r", r=4)[:, :, 0:3], [B_F[1]], [B_mod])
    MM(F[1][:, 0:72], brow[0:72, :], identf[0:72, 0:72], True, True, [B_par, B_c, B_mod], [B_F[1]])
    CP("dve", bT[:], F[1][:, 0:72], [B_F[1]], [B_mod])
    MM(F[1][:, 0:24], grow[0:24, :], identf[0:24, 0:24], True, True, [B_par, B_c], [B_F[1]])
    CP("dve", gT[:], F[1][:, 0:24], [B_F[1]], [B_mod])
    for cch in range(8):
        MM(F[1][:, cch * 16:cch * 16 + 9], cwrow[0:9, cch * 128:(cch + 1) * 128], identf[0:9, 0:9], True, True,
           [B_par, B_c], [B_F[1]])
    CP("dve", cw[:], F[1][:, 0:128].rearrange("p (c t) -> p c t", t=16)[:, :, 0:9], [B_F[1]], [B_mod])
    MM(F[1][:, 0:8], bsrow[0:8, :], identf[0:8, 0:8], True, True, [B_par, B_c], [B_F[1]])
    TS("dve", bsT[:], F[1][:, 0:8], 0.5, None, ALU.mult, None, [B_F[1]], [B_mod])
    MM(F[1][:, :], onesf[0:1, 0:128], cmrow[0:1, :], True, True, [B_par, B_c], [B_F[1]])
    CP("dve", CMG[:], F[1][:, :], [B_F[1]], [B_mod])
    TS("dve", wsb[:], wssb[:], 0.5, None, ALU.mult, None, [B_par], [B_mod])
    for hh in range(8):
        S.op("pe", (lambda hh: lambda h: h.transpose(out=TP[0][:, hh * 128:(hh + 1) * 128], in_=wsb[:, hh, :], identity=identb[:]))(hh),
             [B_mod, B_c], [B_TP[0]])
    CP("dve", wsT[:].rearrange("p h i -> p (h i)"), TP[0][:, :], [B_TP[0]], [B_mod])

    B_wa = [Buf("wa0"), Buf("wa1")]
    waf = arena[:, 0:2 * 16384].bitcast(F32)
    wa = [waf[:, i * 8192:(i + 1) * 8192].rearrange("p (k c) -> p k c", k=8) for i in range(2)]
    MTv = F[0][:, 0:288].rearrange("p (n r) -> p n r", r=4)
    mrow = takeP([3, 512])
    B_mrow = Buf("mrow")
    for m in range(9):
        for k in range(8):
            DMA("sp", wa[m % 2][:, k, :], w_ada[k * 128:(k + 1) * 128, m * D:(m + 1) * D], [], [B_wa[m % 2]], B_wa[m % 2])
        for hf in range(2):
            for k in range(8):
                MM(F[1][0:3, :], scT[:, k, 0:3], wa[m % 2][:, k, hf * 512:(hf + 1) * 512], k == 0, k == 7,
                   [B_wa[m % 2], B_mod], [B_F[1]])
            CP("act", mrow[0:3, :], F[1][0:3, :], [B_F[1]], [B_mrow])
            for cc in range(4):
                n = m * 8 + hf * 4 + cc
                MM(MTv[:, n, 0:3], mrow[0:3, cc * 128:(cc + 1) * 128], identf[0:3, 0:3], True, True, [B_mrow, B_c], [B_F[0]])
    for r in range(3):
        TT("dve", mods[:, :, r], MTv[:, :, r], bT[:], ALU.add, [B_F[0], B_mod], [B_mod])
    for n in range(3):
        for r in range(3):
            TS("dve", AB[:, n, r, 0, :], mods[:, (3 * n + 1) * 8:(3 * n + 2) * 8, r], 1.0, None, ALU.add, None, [B_mod], [B_mod])
            TT("dve", AB[:, n, r, 0, :], AB[:, n, r, 0, :], gT[:, n * 8:(n + 1) * 8], ALU.mult, [B_mod], [B_mod])
            CP("dve", AB[:, n, r, 1, :], mods[:, (3 * n) * 8:(3 * n + 1) * 8, r], [B_mod], [B_mod])

    dg = [sb("dg%d" % i, [128, 128]) for i in range(2)]
    B_dg = [Buf("dg0"), Buf("dg1")]

    def make_G(slot, gate_idx, r, coef):
        for k in range(8):
            TS("dve", dg[k % 2][:], identf[:], mods[:, gate_idx * 8 + k, r:r + 1], coef, ALU.mult, ALU.mult,
               [B_c, B_mod], [B_dg[k % 2]])
            MM(F[2 + k // 4][:, (k % 4) * 128:(k % 4 + 1) * 128], onesf[:], dg[k % 2][:], True, True,
               [B_c, B_dg[k % 2]], [B_F[2 + k // 4]])
        CP("act", G[:, slot, 0:512], F[2][:, :], [B_F[2]], [B_G])
        CP("act", G[:, slot, 512:1024], F[3][:, :], [B_F[3]], [B_G])

    def make_FG(slot):
        for hf in range(2):
            MM(F[2 + hf][:, :], onesf[0:1, 0:128], fgrow[0:1, hf * 512:(hf + 1) * 512], True, True, [B_par, B_c], [B_F[2 + hf]])
            CP("act", FG[:, hf * 512:(hf + 1) * 512], F[2 + hf][:, :], [B_F[2 + hf]], [B_mod])

    cl_i = [0]

    def cast_load(dst, src, view, wbufs, gslot=None):
        i = cl_i[0] % 2
        cl_i[0] += 1
        stv = view(xt[i][:, :, :].rearrange("p s d -> p (s d)"))
        DMA("sp", stv, src, [], [B_xt[i]], B_xt[i])
        if gslot is None:
            CP("act" if i == 0 else "dve", dst, stv, [B_xt[i]], wbufs)
        elif len(dst.shape) == 3:
            for jj in range(dst.shape[1]):
                TT("dve", dst[:, jj, :], stv[:, jj, :], G[:, gslot, :], ALU.mult, [B_xt[i], B_G], wbufs)
        else:
            TT("dve", dst, stv, G[:, gslot, :], ALU.mult, [B_xt[i], B_G], wbufs)

    def conv_w2(f, r, gslot):
        for g in range(11):
            i = cvt_i[0] % 2
            cvt_i[0] += 1
            cast_load(wst[i][:, :].rearrange("p (j c) -> p j c", j=2),
                      ff_out[f][g * 256:(g + 1) * 256, :].rearrange("(j p) c -> p j c", p=128),
                      lambda fl: fl.rearrange("p (j c) -> p j c", j=2), [PW(B_wst[i])], gslot=gslot)
            DMA("sp", w2s[f][r][g * 256:(g + 1) * 256, :].rearrange("(j p) c -> p j c", p=128),
                wst[i][:, :].rearrange("p (j c) -> p j c", j=2), [B_wst[i]], [B_w2s[f][r]], B_wst[i])

    def wcols(g):
        if g < 4:
            return g * 256
        if g < 6:
            return 1056 + (g - 4) * 256
        if g < 8:
            return 1568 + (g - 6) * 256
        return 2080 + (g - 8) * 256

    def conv_win():
        for g in range(10):
            i = cvt_i[0] % 2
            cvt_i[0] += 1
            c0 = wcols(g)
            cast_load(wst[i][:, :].rearrange("p (k c) -> p k c", k=8),
                      w_in[:, c0:c0 + 256].rearrange("(k p) c -> p k c", p=128),
                      lambda fl: fl.rearrange("p (k c) -> p k c", k=8), [B_wst[i]])
            DMA("sp", wins[g], wst[i][:, :], [B_wst[i]], [B_wins], B_wst[i])

    def load_W1(f):
        for k in range(8):
            for (c0, cn) in ((0, 2048), (2048, 2048), (4096, 1536)):
                cast_load(W1[:, k, c0:c0 + cn], ff_in[f][k * 128:(k + 1) * 128, c0:c0 + cn],
                          (lambda cn: lambda fl: fl[:, 0:cn])(cn), [PW(B_W1)])

    wglr = sb("wglr", [128, 8, 32], BF16)

    xt = [sb("xt%d" % i, [128, 2, D]) for i in range(2)]
    B_xt = [Buf("xt%d" % i) for i in range(2)]
    xn = takeA([128, 2, D], BF16)
    B_xn = Buf("xn")
    yT = [takeA([128, 8, T], BF16) for i in range(2)]
    B_yT = [Buf("yT%d" % i) for i in range(2)]
    st = [sb("st%d" % i, [128, 8]) for i in range(4)]
    B_st = [Buf("st%d" % i) for i in range(4)]
    st_i = [0]
    sg = [takeA([128, T]) for i in range(3)]
    B_sg = [Buf("sg%d" % i) for i in range(3)]
    hT = [takeA([128, T], BF16) for i in range(4)]
    B_hT = [Buf("hT%d" % i) for i in range(4)]
    GUB = [F[0][:, :], F[1][:, :], TP[1][:, :].bitcast(F32)]
    B_GUB = [B_F[0], B_F[1], B_TP[1]]
    wst = [takeA([128, 2048], BF16) for i in range(2)]
    B_wst = [Buf("wst%d" % i) for i in range(2)]
    wst_i = [0]
    cvt_i = [0]
    wstx = [G[:, i, :].bitcast(BF16) for i in range(3)]
    B_wstx = [Buf("wstx%d" % i) for i in range(3)]
    wpool = wst + wstx
    B_wpool = B_wst + B_wstx
    NWP = len(wpool)

    xt_i = [0]
    yT_i = [0]
    tp_i = [0]

    def rstd_of(ssap, n_el, eps, reads):
        i = st_i[0] % 4
        st_i[0] += 1
        w = ssap.shape[1]
        TS("dve", st[i][:, 0:w], ssap, 1.0 / n_el, eps, ALU.mult, ALU.add, reads, [B_st[i]])
        ACT(st[i][:, 0:w], st[i][:, 0:w], AF.Ln, [B_st[i]], [B_st[i]])
        ACT(st[i][:, 0:w], st[i][:, 0:w], AF.Exp, [B_st[i]], [B_st[i]], scale=-0.5)
        return st[i][:, 0:w], B_st[i]

    def norm_to_yT(xtile, B_x, n, r, yi, two_tp=False):
        i = st_i[0] % 4
        st_i[0] += 1
        ss, B_ss = st[i][:, 4:6], B_st[i]
        MS("pool", ss, 0.0, [B_ss])
        for s in range(2):
            ACT(tmp[:, :], xtile[:, s, :], AF.Square, [B_x, B_ss], [PW(B_tmp), B_ss], accum=ss[:, s:s + 1])
        rs, B_rs = rstd_of(ss, float(D), EPS, [B_ss])
        TS("dve", xn[:, 0, :], xtile[:, 0, :], rs[:, 0:1], None, ALU.mult, None, [B_x, B_rs], [PW(B_xn)])
        ACT(xn[:, 1, :], xtile[:, 1, :], AF.Identity, [B_x, B_rs], [PW(B_xn)], scale=rs[:, 1:2])
        for s in range(2):
            ti = s if two_tp else 0
            for k in range(8):
                S.op("pe", (lambda ti, k, s: lambda h: h.transpose(out=TP[ti][:, k * 128:(k + 1) * 128],
                                                                 in_=xn[:, s, k * 128:(k + 1) * 128], identity=identb[:]))(ti, k, s),
                     [B_xn, B_c], [B_TP[ti]])
            for k in range(8):
                if s == 0:
                    TS("dve", yT[yi][:, k, s * 128:(s + 1) * 128], TP[ti][:, k * 128:(k + 1) * 128],
                       AB[:, n, r, 0, k:k + 1], AB[:, n, r, 1, k:k + 1], ALU.mult, ALU.add, [B_TP[ti], B_mod], [PW(B_yT[yi])])
                else:
                    ACT(yT[yi][:, k, s * 128:(s + 1) * 128], TP[ti][:, k * 128:(k + 1) * 128], AF.Identity,
                        [B_TP[ti], B_mod], [PW(B_yT[yi])], scale=AB[:, n, r, 0, k:k + 1], bias=AB[:, n, r, 1, k:k + 1])
        return yT[yi], B_yT[yi]

    def ffn(y, B_y, f, r, xtile, B_x, inject=None, inject_at=9):
        FFv = [g_.rearrange("p (a t) -> p a t", a=2) for g_ in GUB]
        wcur = [None]

        def GU(j):
            fb = j % 3
            for a in range(2):
                for k in range(8):
                    MM(FFv[fb][:, a, :], W1[:, k, a * DFF + j * 128:a * DFF + (j + 1) * 128], y[:, k, :], k == 0, k == 7,
                       [B_W1, B_y], [B_GUB[fb]])
            ACT(sg[fb][:], FFv[fb][:, 0, :], AF.Silu, [B_GUB[fb]], [B_sg[fb]])
            TT("dve", hT[j % 4][:], sg[fb][:], FFv[fb][:, 1, :], ALU.mult, [B_sg[fb], B_GUB[fb]], [B_hT[j % 4]])

        def OUT(j):
            if j % 2 == 0:
                wi = wst_i[0] % NWP
                wst_i[0] += 1
                g = j // 2
                DMA("sp", wpool[wi][:, :].rearrange("p (j c) -> p j c", j=2),
                    w2s[f][r][g * 256:(g + 1) * 256, :].rearrange("(j p) c -> p j c", p=128), [B_w2s[f][r]], [B_wpool[wi]], B_wpool[wi])
                wcur[0] = wi
            wi = wcur[0]
            wv = wpool[wi][:, :].rearrange("p (j c) -> p j c", j=2)
            for s in range(2):
                for hf in range(2):
                    MM(F[2 + s * 2 + hf][:, :], hT[j % 4][:, s * 128:(s + 1) * 128], wv[:, j % 2, hf * 512:(hf + 1) * 512],
                       j == 0, j == NFF - 1, [B_hT[j % 4], B_wpool[wi]], [B_F[2 + s * 2 + hf]])

        GU(0)
        GU(1)
        for j in range(NFF):
            if j + 2 < NFF:
                GU(j + 2)
            OUT(j)
            if inject is not None:
                if isinstance(inject, list):
                    if 2 <= j <= 19:
                        per_ = (len(inject) + 17) // 18
                        for th in inject[(j - 2) * per_:(j - 1) * per_]:
                            th()
                elif isinstance(inject, dict):
                    if j in inject:
                        inject[j]()
                elif j == inject_at:
                    inject()
        for s in range(2):
            for hf in range(2):
                TT("dve", xtile[:, s, hf * 512:(hf + 1) * 512], F[2 + s * 2 + hf][:, :], xtile[:, s, hf * 512:(hf + 1) * 512], ALU.add,
                   [B_F[2 + s * 2 + hf], B_x], [PW(B_x)])

    zT = takeA([128, 12, T], BF16)
    B_zT = Buf("zT")
    bmT = takeA([128, 4, T], BF16)
    B_bmT = Buf("bmT")
    tmp = bmT.rearrange("p c t -> p (c t)")
    B_tmp = B_bmT
    glro = takeA([32, T])
    B_glro = Buf("glro")
    us = takeA([128, 512])
    gt = takeA([128, 512])
    gu = takeA([128, 512])
    gv = takeA([128, 512])
    B_us, B_gt, B_gu, B_gv = Buf("us"), Buf("gt"), Buf("gu"), Buf("gv")
    vsn = takeA([128, 512], BF16)
    B_vsn = Buf("vsn")
    bmk = takeA([128, 512], BF16)
    B_bmk = Buf("bmk")

    def gelu2(dst, B_dst, src_ps, B_src):
        CP("act", us[:], src_ps, [B_src], [B_us])
        TT("dve", gt[:], us[:], us[:], ALU.mult, [B_us], [B_gt])
        TS("dve", gt[:], gt[:], 0.044715, 1.0, ALU.mult, ALU.add, [B_gt], [B_gt])
        TT("dve", gt[:], gt[:], us[:], ALU.mult, [B_gt, B_us], [B_gt])
        ACT(gt[:], gt[:], AF.Tanh, [B_gt], [B_gt], scale=0.7978845608028654)
        STT("dve", dst, gt[:], 1.0, us[:], ALU.add, ALU.mult, [B_gt, B_us], [B_dst])

    def win_groups(y2, B_y2, b, tok0, is_ctx):
        FFv = [F[0][:, :].rearrange("p (a t) -> p a t", a=2), F[1][:, :].rearrange("p (a t) -> p a t", a=2)]

        def group(g):
            wi = wst_i[0] % NWP
            wst_i[0] += 1
            DMA("sp", wpool[wi][:, :], wins[g], [B_wins], [B_wpool[wi]], B_wpool[wi])
            wv = wpool[wi][:, :].rearrange("p (k c) -> p k c", k=8)
            if g < 6:
                fb = g % 2
                for cc in range(2):
                    for k in range(8):
                        MM(FFv[fb][:, cc, :], wv[:, k, cc * 128:(cc + 1) * 128], y2[:, k, :], k == 0, k == 7,
                           [B_wpool[wi], B_y2], [B_F[fb]])
                if g < 4:
                    CP("act" if g % 2 == 0 else "dve", zT[:, 2 * g:2 * g + 2, :], FFv[fb][:, :, :], [B_F[fb]], [PW(B_zT)])
                else:
                    ACT(zT[:, 2 * g:2 * g + 2, :], FFv[fb][:, :, :], AF.Silu, [B_F[fb]], [PW(B_zT)])
            else:
                which = (g - 6) // 2
                hf = (g - 6) % 2
                for s in range(2):
                    fi = 2 + which * 2 + s
                    for k in range(8):
                        MM(F[fi][:, hf * 256:(hf + 1) * 256], y2[:, k, s * 128:(s + 1) * 128], wv[:, k, :], k == 0, k == 7,
                           [B_wpool[wi], B_y2], [B_F[fi]])

        def glr_part():
            for k in range(8):
                MM(F[0][0:32, 0:T], wglr[:, k, :], y2[:, k, :], k == 0, k == 7, [B_par, B_y2], [B_F[0]])
            CP("act", glro[:], F[0][0:32, 0:T], [B_F[0]], [B_glro])

        gu2 = xn[:, 0, :].bitcast(F32)
        vsn2 = xn[:, 1, 0:512]
        GUS = [(gu, B_gu, vsn, B_vsn), (gu2, B_xn, vsn2, B_xn)]

        def cm1(s):
            gu, B_gu, vsn, B_vsn = GUS[s]
            gelu2(gu[:], B_gu, F[2 + s][:, :], B_F[2 + s])
            gelu2(gv[:], B_gv, F[4 + s][:, :], B_F[4 + s])
            i = st_i[0] % 4
            st_i[0] += 1
            ss, B_ss = st[i][:, 4:5], B_st[i]
            MS("pool", ss, 0.0, [B_ss])
            ACT(gt[:], gv[:], AF.Square, [B_gv, B_ss], [B_gt, B_ss], accum=ss)
            rs, B_rs = rstd_of(ss, 512.0, 4.0 * EPS, [B_ss])
            STT("dve", vsn[:], gv[:], rs[:, 0:1], CMG[:], ALU.mult, ALU.mult, [B_gv, B_rs, B_mod], [PW(B_vsn)] if s else [B_vsn])

        def cm2(s):
            gu, B_gu, vsn, B_vsn = GUS[s]
            fb = s % 2
            for hh in range(8):
                MM(F[fb][:, hh * 64:(hh + 1) * 64], wsT[:, hh, :], vsn[:, hh * 64:(hh + 1) * 64], True, True,
                   [B_mod, B_vsn], [B_F[fb]])
            for hh in range(8):
                STT("dve", bmk[:, hh * 64:(hh + 1) * 64], F[fb][:, hh * 64:(hh + 1) * 64], bsT[:, hh:hh + 1],
                    gu[:, hh * 64:(hh + 1) * 64], ALU.add, ALU.mult, [B_F[fb], B_mod, B_gu], [B_bmk])
            ti = 0
            for cc in range(4):
                S.op("pe", (lambda ti, cc: lambda h: h.transpose(out=TP[ti][:, cc * 128:(cc + 1) * 128],
                                                              in_=bmk[:, cc * 128:(cc + 1) * 128], identity=identb[:]))(ti, cc),
                     [B_bmk, B_c], [B_TP[ti]])
            CP("act", bmT[:, :, s * 128:(s + 1) * 128], TP[ti][:, 0:512].rearrange("p (c t) -> p c t", c=4), [B_TP[ti]], [B_bmT])

        if is_ctx:
            for g in range(4):
                group(g)
            glr_part()
            DMA("sp", glrcs[b][:, :], glro[:], [B_glro], [B_glrcs[b]], B_glro)
            DMA("sp", zcs[b].rearrange("c p t -> p c t"), zT[:, 0:8, :], [B_zT], [B_zcs[b]], B_zT)
            return
        for g in (6, 7, 8, 9):
            group(g)
        S.cap = []
        cm1(0)
        cm1(1)
        thunks, S.cap = S.cap, None
        per = (len(thunks) + 5) // 6
        for g in range(6):
            group(g)
            for th in thunks[g * per:(g + 1) * per]:
                th()
        for th in thunks[6 * per:]:
            th()
        glr_part()
        DMA("sp", glrs[b][:, tok0:tok0 + T], glro[:], [B_glro], [B_glrs[b]], B_glro)
        DMA("sp", zs[b][:, :, tok0:tok0 + T].rearrange("c p t -> p c t"), zT[:, :, :], [B_zT], [B_zs[b]], B_zT)
        cm2(0)
        cm2(1)
        DMA("sp", bms[b][:, :, tok0:tok0 + T].rearrange("c p t -> p c t"), bmT[:, :, :], [B_bmT], [B_bms[b]], B_bmT)

    def load_x(src_ap, rbufs, xi):
        DMA("sp", xt[xi][:, :, :], src_ap.rearrange("(s p) d -> p s d", p=128), rbufs, [B_xt[xi]], B_xt[xi])

    def run_phaseA(tiles):
        n = len(tiles)
        load_x(tiles[0][0], [], 0)
        norm_to_yT(xt[0], B_xt[0], 0, tiles[0][3], 0)
        for i, (src, b, tok0, r, is_ctx) in enumerate(tiles):
            xi = i % 2

            def inj(i=i):
                if i + 1 < n:
                    load_x(tiles[i + 1][0], [], (i + 1) % 2)
                    norm_to_yT(xt[(i + 1) % 2], B_xt[(i + 1) % 2], 0, tiles[i + 1][3], (i + 1) % 2)
            S.cap = []
            inj()
            thunks, S.cap = S.cap, None
            ffn(yT[xi], B_yT[xi], 0, r, xt[xi], B_xt[xi], inject=thunks)
            if not is_ctx:
                DMA("sp", x1s[b][tok0:tok0 + T, :].rearrange("(s p) d -> p s d", p=128), xt[xi][:, :, :], [B_xt[xi]], [B_x1s[b]], B_xt[xi])
            y2, B_y2 = norm_to_yT(xt[xi], B_xt[xi], 1, r, xi, two_tp=True)
            win_groups(y2, B_y2, b, tok0, is_ctx)

    def run_phaseC2(tiles):
        n = len(tiles)
        b0, t0 = tiles[0]
        load_x(x2s[b0][t0:t0 + T, :], [B_x2s[b0]], 0)
        norm_to_yT(xt[0], B_xt[0], 2, b0, 0)
        fin_rs = {}

        def fin_act(i):
            xi = i % 2
            k = st_i[0] % 4
            st_i[0] += 1
            ss, B_ss = st[k][:, 4:6], B_st[k]
            MS("pool", ss, 0.0, [B_ss])
            for s in range(2):
                ACT(tmp[:, :], xt[xi][:, s, :], AF.Square, [B_xt[xi], B_ss], [PW(B_tmp), B_ss], accum=ss[:, s:s + 1])
            fin_rs[i] = rstd_of(ss, float(D), EPS, [B_ss])

        def fin_dve(i):
            xi = i % 2
            b, tok0 = tiles[i]
            rs, B_rs = fin_rs[i]
            for s in range(2):
                STT("dve", xt[xi][:, s, :], xt[xi][:, s, :], rs[:, s:s + 1], FG[:, :], ALU.mult, ALU.mult,
                    [B_xt[xi], B_rs, B_mod], [PW(B_xt[xi])])
            DMA("sp", out[b][tok0:tok0 + T, :].rearrange("(s p) d -> p s d", p=128), xt[xi][:, :, :], [B_xt[xi]], [], B_xt[xi], final=True)

        for i, (b, tok0) in enumerate(tiles):
            xi = i % 2
            inj = {}
            if i > 0:
                inj[2] = (lambda i=i: fin_act(i - 1))
                inj[6] = (lambda i=i: fin_dve(i - 1))

            def inj_norm(i=i):
                if i + 1 < n:
                    bn, tn = tiles[i + 1]
                    load_x(x2s[bn][tn:tn + T, :], [B_x2s[bn]], (i + 1) % 2)
                    norm_to_yT(xt[(i + 1) % 2], B_xt[(i + 1) % 2], 2, bn, (i + 1) % 2)
            S.cap = []
            if i > 0:
                fin_act(i - 1)
                fin_dve(i - 1)
            inj_norm()
            thunks, S.cap = S.cap, None
            ffn(yT[xi], B_yT[xi], 1, b, xt[xi], B_xt[xi], inject=thunks)
        fin_act(n - 1)
        fin_dve(n - 1)

    zflat = takeB([128, 8 * (ROWS + 2) * 66], BF16)
    zin = [zflat.rearrange("p (c r q) -> p c r q", c=8, r=ROWS + 2)]
    B_zin = [Buf("zin0")]
    zinc = zflat[:, 0:8 * (CTX + 2)].rearrange("p (c r q) -> p c r q", c=8, r=1)
    B_zinc = B_zin[0]
    zraw = takeB([128, 8, (ROWS + 2) * 64], BF16)
    B_zraw = Buf("zraw")
    qkvT = takeB([128, 8, T], BF16)
    B_qkvT = Buf("qkvT")
    glrT = [takeB([33, T]) for i in range(2)]
    B_glrT = [Buf("glrT%d" % i) for i in range(2)]
    ee = takeB([128, 512])
    gneg = ee
    Ep = takeB([128, 4, 128])
    Em = takeB([128, 4, 128])
    B_ee, B_Ep, B_Em = Buf("ee"), Buf("Ep"), Buf("Em")
    B_gneg = B_ee
    qk = [sb("qk%d" % i, [128, 4, 2, T], BF16) for i in range(2)]
    B_qk = [Buf("qk%d" % i) for i in range(2)]
    tokm = [takeB([128, 1024], BF16)]
    B_tokm = [Buf("tokm0")]
    zin_i = [0]
    tok_i = [0]
    qk_i = [0]
    DG = sb("DG", [128, 8, 9, 128], BF16)
    for cch in range(8):
        for tap in range(9):
            TS("dve", DG[:, cch, tap, :], identf[:], cw[:, cch, tap:tap + 1], None, ALU.mult, None, [B_c, B_mod], [PW(B_mod)])

    NA = (NCH + 1) * 256
    arf = arena[:, 0:4 * NA].bitcast(F32)
    ARR = arf[:, 0:NA].rearrange("p (n q v) -> p n q v", q=2, v=128)
    BRR = arf[:, NA:2 * NA].rearrange("p (n q v) -> p n q v", q=2, v=128)
    wo_off = 4 * NA
    Wout = arena[:, wo_off:wo_off + 8 * D].rearrange("p (k c) -> p k c", k=8)
    assert wo_off + 8 * D <= 8 * 2 * DFF
    B_ARR, B_BRR, B_Wout = Buf("ARR"), Buf("BRR"), Buf("Wout")
    ARRc = takeB([128, 3, 2, 128])
    BRRc = takeB([128, 3, 2, 128])
    B_ARRc, B_BRRc = Buf("ARRc"), Buf("BRRc")
    DFt = sb("DF", [128, NCH, 2])
    DBt = sb("DB", [128, NCH, 2])
    DFc = sb("DFc", [128, 2, 2])
    DBc = sb("DBc", [128, 2, 2])
    B_DF, B_DB, B_DFc, B_DBc = Buf("DF"), Buf("DB"), Buf("DFc"), Buf("DBc")

    def geomB(t, is_ctx):
        if is_ctx:
            return 1, CTX, 1, 0, CTX
        tok0 = t * T
        lo = tok0 - 64 if t > 0 else tok0
        hi = tok0 + T + 64 if t < NT - 1 else tok0 + T
        r_lo = 0 if t > 0 else 1
        return ROWS, 64, r_lo, lo, hi

    def B_load(b, t, is_ctx):
        R, C, r_lo, lo, hi = geomB(t, is_ctx)
        gi = t % 2
        if is_ctx:
            DMA("sp", zraw[:, :, 0:CTX], zcs[b].rearrange("c p t -> p c t"), [B_zcs[b]], [B_zraw], B_zraw)
            DMA("sp", glrT[gi][0:32, :], glrcs[b][:, :], [B_glrcs[b]], [B_glrT[gi]], B_glrT[gi])
        else:
            DMA("sp", zraw[:, :, 0:hi - lo], zs[b][0:8, :, lo:hi].rearrange("c p t -> p c t"), [B_zs[b]], [B_zraw], B_zraw)
            DMA("sp", glrT[gi][0:32, :], glrs[b][:, t * T:(t + 1) * T], [B_glrs[b]], [B_glrT[gi]], B_glrT[gi])

    def B_pad(b, t, is_ctx):
        R, C, r_lo, lo, hi = geomB(t, is_ctx)
        zi, B_z = (zinc, B_zinc) if is_ctx else (zin[0], B_zin[0])
        if is_ctx:
            for cch in range(8):
                CP("pool", zi[:, cch, 0:1, 1:C + 1], zraw[:, cch, 0:C].rearrange("p (r q) -> p r q", q=C), [B_zraw], [PW(B_z)])
            return
        nr = (hi - lo) // 64
        if t == 0:
            MS("pool", zi[:, :, 0, :], 0.0, [PW(B_z)])
        if t == NT - 1:
            MS("pool", zi[:, :, R + 1, :], 0.0, [PW(B_z)])
        for cch in range(8):
            CP("pool", zi[:, cch, r_lo:r_lo + nr, 1:65], zraw[:, cch, 0:nr * 64].rearrange("p (r q) -> p r q", q=64),
               [B_zraw], [PW(B_z)])

    def B_conv(b, t, is_ctx):
        R, C, r_lo, lo, hi = geomB(t, is_ctx)
        zi, B_z = (zinc, B_zinc) if is_ctx else (zin[0], B_zin[0])
        taps = [(1, 0), (1, 1), (1, 2)] if is_ctx else [(dr, dc) for dr in range(3) for dc in range(3)]
        for g in range(4):
            for cc in range(2):
                cch = 2 * g + cc
                o_ = F[2 + g][:, cc * 256:(cc + 1) * 256].rearrange("p (r q) -> p r q", q=C)
                for ti_, (dr, dc) in enumerate(taps):
                    view = zi[:, cch, 0:1, dc:dc + C] if is_ctx else zi[:, cch, dr:dr + R, dc:dc + C]
                    MM(o_, DG[:, cch, dr * 3 + dc, :], view, ti_ == 0, ti_ == len(taps) - 1, [B_z, B_mod], [B_F[2 + g]])
            ACT(qkvT[:, 2 * g:2 * g + 2, :], F[2 + g][:, :].rearrange("p (a t) -> p a t", a=2), AF.Silu, [B_F[2 + g]], [PW(B_qkvT)])

    def B_gates(b, t, is_ctx):
        gi = t % 2
        qi = qk_i[0] % 2
        qk_i[0] += 1

        def front(s):
            n = 2 * t + s
            sl = slice(s * 128, (s + 1) * 128)
            MM(F[0][:, :], glrT[gi][0:33, sl], Wg[0:33, :], True, True, [B_glrT[gi], B_par], [B_F[0]])
            ACT(ee[:], F[0][:, :], AF.Exp, [B_F[0]], [B_ee], scale=-1.0)
            ACT(gneg[:], ee[:], AF.Ln, [B_ee], [B_gneg], bias=1.0)
            for cc in range(4):
                MM(F[1][:, cc * 128:(cc + 1) * 128], gneg[:, cc * 128:(cc + 1) * 128], (trif if cc < 2 else trib)[:], True, True,
                   [B_gneg, B_c], [B_F[1]])
            F1v = F[1][:, :].rearrange("p (c t) -> p c t", c=4)
            ACT(Ep[:], F1v, AF.Exp, [B_F[1]], [B_Ep])
            ACT(Em[:], F1v, AF.Exp, [B_F[1]], [B_Em], scale=-1.0)
            STT("dve", qk[qi][:, 0, :, sl], qkvT[:, 0:2, sl], 0.125, Ep[:, 0:2, :], ALU.mult, ALU.mult, [B_qkvT, B_Ep], [PW(B_qk[qi])])
            STT("dve", qk[qi][:, 1, :, sl], qkvT[:, 0:2, sl], 0.125, Ep[:, 2:4, :], ALU.mult, ALU.mult, [B_qkvT, B_Ep], [PW(B_qk[qi])])
            TT("dve", qk[qi][:, 2, :, sl], qkvT[:, 2:4, sl], Em[:, 0:2, :], ALU.mult, [B_qkvT, B_Em], [PW(B_qk[qi])])
            TT("dve", qk[qi][:, 3, :, sl], qkvT[:, 2:4, sl], Em[:, 2:4, :], ALU.mult, [B_qkvT, B_Em], [PW(B_qk[qi])])
            if is_ctx:
                CP("dve", DFc[:, n, :], Ep[:, 0:2, 127], [B_Ep], [PW(B_DFc)])
                CP("dve", DBc[:, n, :], Ep[:, 2:4, 0], [B_Ep], [PW(B_DBc)])
            else:
                CP("dve", DFt[:, n, :], Ep[:, 0:2, 127], [B_Ep], [PW(B_DF)])
                CP("dve", DBt[:, n, :], Ep[:, 2:4, 0], [B_Ep], [PW(B_DB)])

        def back(s):
            n = 2 * t + s
            sl = slice(s * 128, (s + 1) * 128)
            ti = tp_i[0] % 2
            tp_i[0] += 1
            srcs = [qk[qi][:, 2, 0, sl], qk[qi][:, 2, 1, sl], qk[qi][:, 3, 0, sl], qk[qi][:, 3, 1, sl]] + \
                   [qkvT[:, 4 + cc, sl] for cc in range(4)]
            for cc, sap in enumerate(srcs):
                S.op("pe", (lambda ti, cc, sap: lambda h: h.transpose(out=TP[ti][:, cc * 128:(cc + 1) * 128], in_=sap,
                                                                   identity=identb[:]))(ti, cc, sap),
                     [B_qk[qi], B_qkvT, B_c], [B_TP[ti]])
            ki = 0
            CP("act", tokm[ki][:, :], TP[ti][:, :], [B_TP[ti]], [B_tokm[ki]])
            if not is_ctx:
                DMA("sp", vts[b][n * 128:(n + 1) * 128, :], tokm[ki][:, 512:1024], [B_tokm[ki]], [B_vts[b]], B_tokm[ki])
            for d in range(2):
                for p in range(2):
                    MM(F[2 + d][:, p * 256:(p + 1) * 256], tokm[ki][:, d * 256 + p * 128:d * 256 + (p + 1) * 128],
                       tokm[ki][:, 512 + p * 256:512 + (p + 1) * 256], True, True, [B_tokm[ki]], [B_F[2 + d]])
            A_, B_A = (ARRc, B_ARRc) if is_ctx else (ARR, B_ARR)
            Bk, B_B = (BRRc, B_BRRc) if is_ctx else (BRR, B_BRR)
            for d in range(2):
                dst, B_dst, idx = (A_, B_A, n + 1) if d == 0 else (Bk, B_B, n)
                Fv = F[2 + d][:, :].rearrange("p (q x) -> p q x", q=2)
                e_ = "act" if d == 0 else "dve"
                CP(e_, dst[0:64, idx, :, :], Fv[0:64, :, 0:128], [B_F[2 + d]], [PW(B_dst)])
                CP(e_, dst[64:128, idx, :, :], Fv[64:128, :, 128:256], [B_F[2 + d]], [PW(B_dst)])

        front(0)
        front(1)
        back(0)
        back(1)
        if not is_ctx:
            DMA("sp", qks[b][t], qk[qi][:, :, :, :].rearrange("p a c t -> p (a c t)"), [B_qk[qi]], [B_qks[b]], B_qk[qi])

    def scan(A_, B_A, Bk, B_B, DFx, B_DFx, DBx, B_DBx, nch):
        for n in range(nch):
            TT("dve", A_[:, n + 1, :, :], A_[:, n + 1, :, :], A_[:, n, :, :], ALU.add, [B_A], [B_A])
            for p in range(2):
                TS("dve", A_[:, n + 1, p, :], A_[:, n + 1, p, :], DFx[:, n, p:p + 1], None, ALU.mult, None, [B_A, B_DFx], [B_A])
        for n in range(nch - 1, -1, -1):
            TT("dve", Bk[:, n, :, :], Bk[:, n, :, :], Bk[:, n + 1, :, :], ALU.add, [B_B], [B_B])
            for p in range(2):
                ACT(Bk[:, n, p, :], Bk[:, n, p, :], AF.Identity, [B_B, B_DBx], [B_B], scale=DBx[:, n, p:p + 1])

    vt = [takeC([128, 2, 512], BF16) for i in range(2)]
    B_vt = [Buf("vt%d" % i) for i in range(2)]
    zo = [takeC([128, 4, T], BF16) for i in range(2)]
    B_zo = [Buf("zo%d" % i) for i in range(2)]
    bmi = [takeC([128, 4, T], BF16) for i in range(2)]
    B_bmi = [Buf("bmi%d" % i) for i in range(2)]
    t1 = takeC([128, 4, 128], BF16)
    t2 = takeC([128, 4, 128], BF16)
    B_t1, B_t2 = Buf("t1"), Buf("t2")
    sq = takeC([128, 512], BF16)
    B_sq = Buf("sq")
    msb = takeC([128, 512])
    B_msb = Buf("msb")
    a1 = msb
    B_a1 = B_msb
    aT = takeC([128, 4, 128], BF16)
    B_aT = Buf("aT")
    SFn = takeC([128, 2, 128], BF16)
    SBn = takeC([128, 2, 128], BF16)
    B_SFn, B_SBn = Buf("SFn"), Buf("SBn")
    c1_i = [0]

    t1b = takeC([128, 4, 128], BF16)
    t2b = takeC([128, 4, 128], BF16)
    B_t1b, B_t2b = Buf("t1b"), Buf("t2b")
    OB = [F[2][:, :], TP[0][:, :].bitcast(F32)]
    B_OB = [B_F[2], B_TP[0]]
    T12 = [(t1, B_t1, t2, B_t2), (t1b, B_t1b, t2b, B_t2b)]

    def C1_load(b, t):
        ci = t % 2
        tok0 = t * T
        DMA("sp", qk[ci][:, :, :, :].rearrange("p a c t -> p (a c t)"), qks[b][t], [B_qks[b]], [B_qk[ci]], B_qk[ci])
        DMA("sp", vt[ci][:, :, :], vts[b][tok0:tok0 + T, :].rearrange("(s p) c -> p s c", p=128), [B_vts[b]], [B_vt[ci]], B_vt[ci])
        DMA("sp", zo[ci][:, :, :], zs[b][8:12, :, tok0:tok0 + T].rearrange("c p t -> p c t"), [B_zs[b]], [B_zo[ci]], B_zo[ci])
        DMA("sp", bmi[ci][:, :, :], bms[b][:, :, tok0:tok0 + T].rearrange("c p t -> p c t"), [B_bms[b]], [B_bmi[ci]], B_bmi[ci])
        DMA("sp", xt[ci][:, :, :], x1s[b][tok0:tok0 + T, :].rearrange("(s p) d -> p s d", p=128), [B_x1s[b]], [B_xt[ci]], B_xt[ci])

    def C1_compute(b, t):
        ci = t % 2
        xi = ci
        tok0 = t * T
        q_ = qk[ci]

        def stage1(s):
            n = 2 * t + s
            sl = slice(s * 128, (s + 1) * 128)
            ta, B_ta, tb, B_tb = T12[s]
            CP("act", SFn[:, :, :], ARR[:, n, :, :], [B_ARR], [B_SFn])
            CP("act", SBn[:, :, :], BRR[:, n + 1, :, :], [B_BRR], [B_SBn])
            for half in range(2):
                pr = slice(half * 64, half * 64 + 64)
                for p in range(2):
                    MM(F[half][:, p * 128:(p + 1) * 128], q_[pr, 2, p, sl], q_[pr, 0, p, sl], True, True, [B_qk[ci]], [B_F[half]])
                    MM(F[half][:, (2 + p) * 128:(3 + p) * 128], q_[pr, 3, p, sl], q_[pr, 1, p, sl], True, True, [B_qk[ci]], [B_F[half]])
            TT("dve", ta[:, :, :], F[0][:, :].rearrange("p (h i) -> p h i", h=4), mf4[:, :, :], ALU.mult, [B_F[0], B_c], [B_ta])
            TT("dve", tb[:, :, :], F[1][:, :].rearrange("p (h i) -> p h i", h=4), mf4[:, :, :], ALU.mult, [B_F[1], B_c], [B_tb])
            for hh in range(4):
                p, pr = hh // 2, slice((hh % 2) * 64, (hh % 2) * 64 + 64)
                o_ = OB[s][:, hh * 128:(hh + 1) * 128]
                tm = ta if hh % 2 == 0 else tb
                MM(o_, vt[ci][:, s, hh * 128:(hh + 1) * 128], tm[:, p, :], True, False, [B_vt[ci], B_ta, B_tb], [B_OB[s]])
                MM(o_, vt[ci][:, s, hh * 128:(hh + 1) * 128], tm[:, 2 + p, :], False, False, [B_vt[ci], B_ta, B_tb], [B_OB[s]])
                MM(o_, SFn[pr, p, :], q_[pr, 0, p, sl], False, False, [B_SFn, B_qk[ci]], [B_OB[s]])
                MM(o_, SBn[pr, p, :], q_[pr, 1, p, sl], False, True, [B_SBn, B_qk[ci]], [B_OB[s]])

        def stage2(s):
            sl = slice(s * 128, (s + 1) * 128)
            ACT(sq[:], OB[s], AF.Square, [B_OB[s]], [B_sq])
            MM(F[3][:, :], onesb[:], sq[:], True, True, [B_c, B_sq], [B_F[3]])
            TS("dve", msb[:], F[3][:, :], 1.0 / 128.0, EPS, ALU.mult, ALU.add, [B_F[3]], [B_msb])
            ACT(msb[:], msb[:], AF.Ln, [B_msb], [B_msb])
            ACT(msb[:], msb[:], AF.Exp, [B_msb], [B_msb], scale=-0.5)
            TT("dve", a1[:], OB[s], msb[:], ALU.mult, [B_OB[s], B_msb], [B_a1])
            STT("dve", aT[:, :, :], a1[:].rearrange("p (h i) -> p h i", h=4), glag[:, 0:1], zo[ci][:, :, sl], ALU.mult, ALU.mult,
                [B_a1, B_par, B_zo[ci]], [B_aT])
            for hf in range(2):
                for cc in range(4):
                    MM(F[4 + hf][:, :], aT[:, cc, :], Wout[:, cc, hf * 512:(hf + 1) * 512], cc == 0, False, [B_aT, B_Wout], [B_F[4 + hf]])
                for cc in range(4):
                    MM(F[4 + hf][:, :], bmi[ci][:, cc, sl], Wout[:, 4 + cc, hf * 512:(hf + 1) * 512], False, cc == 3,
                       [B_bmi[ci], B_Wout], [B_F[4 + hf]])
            for hf in range(2):
                TT("dve", xt[xi][:, s, hf * 512:(hf + 1) * 512], F[4 + hf][:, :], xt[xi][:, s, hf * 512:(hf + 1) * 512], ALU.add,
                   [B_F[4 + hf], B_xt[xi]], [PW(B_xt[xi])])

        stage1(0)
        stage1(1)
        stage2(0)
        stage2(1)
        DMA("sp", x2s[b][tok0:tok0 + T, :].rearrange("(s p) d -> p s d", p=128), xt[xi][:, :, :], [B_xt[xi]], [B_x2s[b]], B_xt[xi])

    def finish():
        with nc.allow_low_precision("bf16 matmul operands, fp32 accumulation"), nc.allow_non_contiguous_dma("layout"):
            S.emit()
        print("ops", {e: len(v) for e, v in S.ops.items()}, "dma sems", len(S.dma_bufs))
        return nc

    print("carve", stA, stB, stC, stP, "sbuf_left", nc.sbuf_bytes_remaining)
    assert stA[0] <= YB and stB[0] <= YB and stP[0] <= 8 * 2 * DFF * 2, (stA, stB, stP)
    A_set = [B_xn] + B_yT + B_sg + B_hT + B_wst + [B_zT, B_bmT, B_glro, B_us, B_gt, B_gu, B_gv, B_vsn, B_bmk]
    B_set = [B_zin[0], B_zraw, B_qkvT] + B_glrT + [B_ee, B_Ep, B_Em, B_tokm[0], B_ARRc, B_BRRc]
    C_set = B_vt + B_zo + B_bmi + [B_t1, B_t2, B_t1b, B_t2b, B_sq, B_msb, B_aT, B_SFn, B_SBn]
    assert stC[0] <= YB
    make_FG(2)
    cast_load(wglr[:], w_in[:, 1024:1056].rearrange("(k p) c -> p k c", p=128),
              lambda fl: fl[:, 0:256].rearrange("p (k c) -> p k c", k=8), [B_par])
    fence(B_W1, B_wa + [B_par, B_mod, B_mrow])
    load_W1(0)
    for r in range(3):
        make_G(r, 2, r, 0.5)
    conv_w2(0, 2, 2)
    for b in range(NB):
        conv_w2(0, b, b)
    conv_win()
    if stage == 0:
        return finish()
    for i in range(3):
        fence(B_wstx[i], [B_G])
    tilesA = [(ctx[b], b, 0, 2, True) for b in range(NB)]
    tilesA += [(x[b][t * T:(t + 1) * T, :], b, t * T, b, False) for b in range(NB) for t in range(NT)]
    run_phaseA(tilesA)
    if stage == 2:
        return finish()
    fence(B_ARR, [B_W1])
    fence(B_BRR, [B_W1])
    fence(B_Wout, [B_W1])
    fence(B_G, B_wstx)
    for b in range(NB):
        make_G(b, 8, b, 0.5)
        conv_w2(1, b, b)
    for b in range(NB):
        make_G(b, 5, b, 1.0)
    for b in range(NB):
        for bb in B_set:
            fence(bb, A_set + C_set)
        for k in range(8):
            cast_load(Wout[:, k, :], w_out[k * 128:(k + 1) * 128, :], lambda fl: fl[:, 0:1024], [PW(B_Wout)], gslot=b)
        MS("pool", ARRc[:, 0, :, :], 0.0, [B_ARRc])
        MS("pool", BRRc[:, 2, :, :], 0.0, [B_BRRc])
        for i in range(2):
            MS("pool", glrT[i][:], 1.0, [B_glrT[i]])
        MS("pool", zflat[:, :], 0.0, [B_zin[0]])
        B_load(b, 0, True)
        B_pad(b, 0, True)
        B_conv(b, 0, True)
        B_gates(b, 0, True)
        MS("pool", zflat[:, :], 0.0, [B_zin[0]])
        scan(ARRc, B_ARRc, BRRc, B_BRRc, DFc, B_DFc, DBc, B_DBc, 2)
        CP("dve", ARR[:, 0, :, :], ARRc[:, 2, :, :], [B_ARRc], [B_ARR])
        CP("pool", BRR[:, NCH, :, :], BRRc[:, 0, :, :], [B_BRRc], [B_BRR])
        if stage == 2.3:
            return finish()
        B_load(b, 0, False)
        B_pad(b, 0, False)
        for t in range(NT):
            B_conv(b, t, False)
            if t + 1 < NT:
                B_load(b, t + 1, False)
                B_pad(b, t + 1, False)
            B_gates(b, t, False)
        if stage == 2.4:
            return finish()
        scan(ARR, B_ARR, BRR, B_BRR, DFt, B_DF, DBt, B_DB, NCH)
        if stage == 2.5:
            return finish()
        for bb in C_set:
            fence(bb, A_set + B_set)
        C1_load(b, 0)
        for t in range(NT):
            if t + 1 < NT:
                C1_load(b, t + 1)
            C1_compute(b, t)
    if stage == 3:
        return finish()
    B_W1b = Buf("W1b")
    fence(B_W1b, [B_ARR, B_BRR, B_Wout])
    B_W1.ws = []
    B_W1.open = False
    B_W1.readers = list(B_W1b.readers)
    load_W1(1)
    for bb in A_set:
        fence(bb, B_set + C_set)
    for i in range(3):
        fence(B_wstx[i], [B_G])
    run_phaseC2([(b, t * T) for b in range(NB) for t in range(NT)])

    return finish()


_CACHE = {}


def _core_inputs(inputs, b0, NB, SEQ):
    f = lambda a: np.ascontiguousarray(np.asarray(a, dtype=np.float32))
    g = np.concatenate([np.asarray(inputs["norm1_g"][0]).reshape(8, 128), np.asarray(inputs["norm2_g"][0]).reshape(8, 128),
                        np.asarray(inputs["norm3_g"][0]).reshape(8, 128)], axis=0)
    return {
        "x": f(inputs["x"][b0:b0 + NB, :SEQ]),
        "c": f(inputs["c"][b0:b0 + NB]),
        "ctx": f(inputs["ctx"][b0:b0 + NB]),
        "c_ctx": f(np.asarray(inputs["c_ctx"]).reshape(1, D)),
        "w_ada": f(inputs["w_ada"][0]),
        "b_ada": f(np.asarray(inputs["b_ada"][0]).reshape(72, 128)),
        "gains": f(g),
        "ff1_in": f(inputs["ff1_in"][0]),
        "ff2_in": f(inputs["ff2_in"][0]),
        "ff1_out": f(inputs["ff1_out"][0]),
        "ff2_out": f(inputs["ff2_out"][0]),
        "w_in": f(inputs["w_in"][0]),
        "conv_w": f(np.asarray(inputs["conv_w"][0]).reshape(9, D)),
        "w_gate_f": f(inputs["w_gate_f"][0]),
        "b_gate_f": f(np.asarray(inputs["b_gate_f"][0]).reshape(1, 256)),
        "w_gate_b": f(inputs["w_gate_b"][0]),
        "b_gate_b": f(np.asarray(inputs["b_gate_b"][0]).reshape(1, 256)),
        "gla_norm_g": f(np.asarray(inputs["gla_norm_g"][0]).reshape(128, 1)),
        "cmlp_norm_g": f(np.asarray(inputs["cmlp_norm_g"][0]).reshape(1, 512)),
        "w_s": f(inputs["w_s"][0]),
        "b_s": f(inputs["b_s"][0]),
        "w_out": f(inputs["w_out"][0]),
        "final_g": f(np.asarray(inputs["final_g"]).reshape(1, D)),
    }


def kernel(**inputs):
    x = np.asarray(inputs["x"])
    B, SEQ = x.shape[0], x.shape[1]
    ncores = 8
    NB = B // ncores
    key = (NB, SEQ)
    if key not in _CACHE:
        _CACHE[key] = build(NB, SEQ)
    nc = _CACHE[key]
    in_maps = [_core_inputs(inputs, i * NB, NB, SEQ) for i in range(ncores)]
    res = run_bass_kernel_spmd(nc, in_maps, core_ids=list(range(ncores)))
    return np.concatenate([np.asarray(r["out"]) for r in res.results], axis=0).astype(np.float32)
```
